# Optimizing a Trainium2 kernel written in Bass

```python
import math
import jax, jax.numpy as jnp
from jax import lax
import numpy as np

D_MODEL = 1024
BATCH = 16
SEQ = 2048
DEPTH = 2

N_HEADS = 16
HEAD_DIM = D_MODEL // N_HEADS
N_MIXERS = 2
CHUNK = 64
LEFT_CHUNKS = 8
BAND = (LEFT_CHUNKS + 1) * CHUNK
REL_CLIP = 128
N_REL = (CHUNK - 1) + REL_CLIP + 1
Q_BLOCK = 128
D_FF = int(math.ceil(D_MODEL * 8 / 3 / 256) * 256)
N_A_LAYERS = (DEPTH + N_MIXERS - 1) // N_MIXERS
RMS_EPS = 1e-6

kernel_name = "hybrid_chunkattn_stickbreak_trunk"


def rms_norm(x, g):
    xf = x.astype(jnp.float32)
    y = xf * lax.rsqrt(jnp.mean(xf * xf, axis=-1, keepdims=True) + RMS_EPS)
    return (y * g.astype(jnp.float32)).astype(x.dtype)


def split_heads(h, w_qkv):
    b, s, _ = h.shape
    qkv = (h @ w_qkv).reshape(b, s, 3, N_HEADS, HEAD_DIM)
    qkv = jnp.transpose(qkv, (2, 0, 3, 1, 4))
    return qkv[0], qkv[1], qkv[2]


def merge_heads(o):
    b, h, s, d = o.shape
    return jnp.transpose(o, (0, 2, 1, 3)).reshape(b, s, h * d)


def chunk_relpos_attention(h, w_qkv, w_o, rel_bias):
    b, s, _ = h.shape
    n_chunks = s // CHUNK
    q, k, v = split_heads(h, w_qkv)
    i_idx = np.arange(CHUNK)[:, None]
    j_idx = np.arange(BAND)[None, :]
    rel = np.clip(LEFT_CHUNKS * CHUNK + i_idx - j_idx, -(CHUNK - 1), REL_CLIP) + (CHUNK - 1)
    bias = jnp.take(rel_bias, jnp.asarray(rel, dtype=jnp.int32), axis=1).astype(jnp.float32)
    scale = 1.0 / math.sqrt(HEAD_DIM)
    outs = []
    for c in range(n_chunks):
        k0 = max(0, (c - LEFT_CHUNKS) * CHUNK)
        k1 = (c + 1) * CHUNK
        n_keys = k1 - k0
        q_blk = q[:, :, c * CHUNK:(c + 1) * CHUNK]
        sc = jnp.einsum('bhqd,bhkd->bhqk', q_blk, k[:, :, k0:k1]).astype(jnp.float32) * scale
        sc = sc + bias[None, :, :, BAND - n_keys:]
        p = jax.nn.softmax(sc, axis=-1).astype(v.dtype)
        outs.append(jnp.einsum('bhqk,bhkd->bhqd', p, v[:, :, k0:k1]))
    o = jnp.concatenate(outs, axis=2)
    return merge_heads(o) @ w_o


def stick_breaking_attention(h, w_qkv, w_o):
    b, s, _ = h.shape
    q, k, v = split_heads(h, w_qkv)
    scale = 1.0 / math.sqrt(HEAD_DIM)
    outs = []
    for qb in range(s // Q_BLOCK):
        n_keys = (qb + 1) * Q_BLOCK
        q_blk = q[:, :, qb * Q_BLOCK:(qb + 1) * Q_BLOCK]
        z = jnp.einsum('bhqd,bhkd->bhqk', q_blk, k[:, :, :n_keys]).astype(jnp.float32) * scale
        t_pos = qb * Q_BLOCK + np.arange(Q_BLOCK)
        causal = jnp.asarray((np.arange(n_keys)[None, :] < t_pos[:, None])[None, None])
        log_keep = jnp.where(causal, jax.nn.log_sigmoid(-z), 0.0)
        suffix = lax.cumsum(log_keep, axis=3, reverse=True) - log_keep
        a = jnp.where(causal, jnp.exp(jax.nn.log_sigmoid(z) + suffix), 0.0).astype(v.dtype)
        outs.append(jnp.einsum('bhqk,bhkd->bhqd', a, v[:, :, :n_keys]))
    o = jnp.concatenate(outs, axis=2)
    return merge_heads(o) @ w_o


def swiglu(h, w_gate, w_up, w_down):
    return (jax.nn.silu(h @ w_gate) * (h @ w_up)) @ w_down


def setup_inputs(seed: int = 0) -> dict:
    key = jax.random.key(seed)
    ks = jax.random.split(key, 12)
    d, f = D_MODEL, D_FF
    nrm = lambda k, shape, fan: jax.random.normal(k, shape, jnp.float32) * (fan ** -0.5)
    gain = lambda k: 1.0 + 0.02 * jax.random.normal(k, (DEPTH, d), jnp.float32)
    return {
        "x": jax.random.normal(ks[0], (BATCH, SEQ, d), jnp.float32),
        "g_pre_mix": gain(ks[1]),
        "g_post_mix": gain(ks[2]),
        "w_qkv": nrm(ks[3], (DEPTH, d, 3 * d), d),
        "w_o": nrm(ks[4], (DEPTH, d, d), d),
        "rel_bias": 0.5 * jax.random.normal(ks[5], (N_A_LAYERS, N_HEADS, N_REL), jnp.float32),
        "g_pre_ffn": gain(ks[6]),
        "g_post_ffn": gain(ks[7]),
        "w_gate": nrm(ks[8], (DEPTH, d, f), d),
        "w_up": nrm(ks[9], (DEPTH, d, f), d),
        "w_down": nrm(ks[10], (DEPTH, f, d), f),
    }


def reference(x, g_pre_mix, g_post_mix, w_qkv, w_o, rel_bias, g_pre_ffn, g_post_ffn, w_gate, w_up, w_down):
    for i in range(DEPTH):
        h = rms_norm(x, g_pre_mix[i])
        if i % N_MIXERS == 0:
            m = chunk_relpos_attention(h, w_qkv[i], w_o[i], rel_bias[i // N_MIXERS])
        else:
            m = stick_breaking_attention(h, w_qkv[i], w_o[i])
        x = x + rms_norm(m, g_post_mix[i])
        h = rms_norm(x, g_pre_ffn[i])
        x = x + rms_norm(swiglu(h, w_gate[i], w_up[i], w_down[i]), g_post_ffn[i])
    return x
```

```python
import numpy as np
import concourse.bass as bass
import concourse.mybir as mybir
from concourse.bass_utils import run_bass_kernel_spmd

F32 = mybir.dt.float32
BF16 = mybir.dt.bfloat16
AF = mybir.ActivationFunctionType
ALU = mybir.AluOpType

D = 1024
S = 2048
NH = 16
DFF = 2816
NT = 16
KC = 8
FC = 22
EPS = 1e-6
NEG = -30000.0

ENGS = ("pe", "act", "dve", "pool", "sp")


class Op:
    __slots__ = ("eng", "fn", "deps", "is_dma", "dma_sem", "dma_val", "sig", "idx")

    def __init__(self, eng, fn, deps, is_dma):
        self.eng = eng
        self.fn = fn
        self.deps = deps
        self.is_dma = is_dma
        self.dma_sem = None
        self.dma_val = 0
        self.sig = 0
        self.idx = -1


class Prog:
    def __init__(self, nc):
        self.nc = nc
        self.ops = []
        self.per_eng = {e: [] for e in ENGS}
        self.last_w = {}
        self.readers = {}
        self.dma_count = {}
        self.same_eng_window = 6
        self._n = 0

    def op(self, eng, fn, reads=(), writes=(), dma_slot=None):
        psr = [k for k in reads if isinstance(k, tuple) and k[0] == "ps"]
        if psr:
            reads = [k for k in reads if not (isinstance(k, tuple) and k[0] == "ps")]
            writes = list(writes) + [k for k in psr if k not in writes]
        deps = set()
        for k in reads:
            w = self.last_w.get(k)
            if w is not None:
                deps.add(w)
        for k in writes:
            w = self.last_w.get(k)
            if w is not None:
                deps.add(w)
            rd = self.readers.get(k)
            if rd:
                deps.update(rd.values())
        is_dma = dma_slot is not None
        o = Op(eng, fn, deps, is_dma)
        if is_dma:
            c = self.dma_count.get(dma_slot, 0) + 16
            self.dma_count[dma_slot] = c
            o.dma_sem = dma_slot
            o.dma_val = c
        o.idx = len(self.per_eng[eng])
        self.per_eng[eng].append(o)
        self.ops.append(o)
        for k in writes:
            self.last_w[k] = o
            self.readers[k] = {}
        self._n += 1
        for k in reads:
            rd = self.readers.setdefault(k, {})
            rd[(eng, self._n) if is_dma else eng] = o
        return o

    def emit(self, final_waits=()):
        nc = self.nc
        needed = set()
        for o in self.ops:
            eff = []
            for d in o.deps:
                if not d.is_dma and d.eng == o.eng and not o.is_dma:
                    if o.eng == "pe":
                        continue
                    if o.idx - d.idx > self.same_eng_window:
                        continue
                eff.append(d)
                if not d.is_dma:
                    needed.add(d)
            o.deps = eff
        for o in final_waits:
            if not o.is_dma:
                needed.add(o)
        for e in ENGS:
            c = 0
            for o in self.per_eng[e]:
                if o in needed:
                    c += 1
                    o.sig = c
        sems = {e: nc.alloc_semaphore("prog_" + e) for e in ENGS}
        dsems = {}
        for i, k in enumerate(self.dma_count):
            dsems[k] = nc.alloc_semaphore("dma%d" % i)
        engobj = {"pe": "tensor", "act": "scalar", "dve": "vector", "pool": "gpsimd", "sp": "sync"}
        per_eng = self.per_eng
        stats = {e: [len(per_eng[e]), 0] for e in ENGS}

        def make(e):
            def body(eng):
                waited = {}
                for o in per_eng[e]:
                    for d in o.deps:
                        if d.is_dma:
                            key = ("d", d.dma_sem)
                            if waited.get(key, 0) < d.dma_val:
                                eng.wait_ge(dsems[d.dma_sem], d.dma_val)
                                waited[key] = d.dma_val
                                stats[e][1] += 1
                        else:
                            key = ("e", d.eng)
                            if waited.get(key, 0) < d.sig:
                                eng.wait_ge(sems[d.eng], d.sig)
                                waited[key] = d.sig
                                stats[e][1] += 1
                    ins = o.fn(eng)
                    if o.is_dma:
                        ins.then_inc(dsems[o.dma_sem], 16)
                    elif o.sig:
                        ins.then_inc(sems[e], 1)
                if e == "sp":
                    for o in final_waits:
                        if o.is_dma:
                            eng.wait_ge(dsems[o.dma_sem], o.dma_val)
                        else:
                            eng.wait_ge(sems[o.eng], o.sig)
            return body

        with nc.Block() as block:
            for e in ENGS:
                if per_eng[e] or e == "sp":
                    getattr(block, engobj[e])(make(e))
        self.stats = stats
        return stats


def interleave(main, side, n_side):
    side_done = side is None
    for _ in main:
        if not side_done:
            for _i in range(n_side):
                try:
                    next(side)
                except StopIteration:
                    side_done = True
                    break
    if not side_done:
        for _ in side:
            pass


def build_program(nseq=2, layers=(0, 1), do_ffn=True, do_mix=True):
    nc = bass.Bass("TRN2", target_bir_lowering=False, dynamic_dma_scratch_size=4096)
    dt = lambda name, shape, kind="ExternalInput": nc.dram_tensor(name, shape, F32, kind=kind).ap()
    x_d = dt("x", [nseq, S, D])
    y_d = dt("y", [nseq, S, D], kind="ExternalOutput")
    wqkv_d = dt("w_qkv", [2, D, 3 * D])
    wo_d = dt("w_o", [2, D, D])
    wg_d = dt("w_gate", [2, D, DFF])
    wu_d = dt("w_up", [2, D, DFF])
    wd_d = dt("w_down", [2, DFF, D])
    gains_d = dt("gains", [8, 128, D])
    bias_d = dt("bias_t", [NH, 128, 256])
    cb_d = dt("cbias", [128, NH])
    consts_d = dt("consts", [5, 128, 128])

    P = Prog(nc)

    X = nc.alloc_sbuf_tensor("X", [128, NT, D], F32)
    ARENA_B = 106624
    ARENA = nc.alloc_sbuf_tensor("ARENA", [128, ARENA_B // 2], BF16)
    SLAB = [nc.alloc_sbuf_tensor("slab%d" % i, [128, KC, 256], BF16) for i in range(6)]
    G = [nc.alloc_sbuf_tensor("G%d" % i, [128, D], F32) for i in range(2)]
    TMP = [nc.alloc_sbuf_tensor("tmp%d" % i, [128, 512], F32) for i in range(2)]
    JUNK = nc.alloc_sbuf_tensor("junk", [128, D], BF16)
    XS = [nc.alloc_sbuf_tensor("xs%d" % i, [128, D], BF16) for i in range(2)]
    CF = nc.alloc_sbuf_tensor("cf", [128, 5, 128], F32)
    IDB = nc.alloc_sbuf_tensor("idb", [128, 128], BF16)
    TRI = nc.alloc_sbuf_tensor("tri", [128, 128], BF16)
    TRIC = nc.alloc_sbuf_tensor("tric", [128, 128], BF16)
    CB = nc.alloc_sbuf_tensor("cb", [128, NH], F32)
    ST = nc.alloc_sbuf_tensor("st", [128, 16], F32)
    MASK2 = CF[:, 3:5, :]

    def aview(off, shape, dtype=BF16):
        n = 1
        for s_ in shape[1:]:
            n *= s_
        esz = 2 if dtype == BF16 else 4
        ap = ARENA[:, off // 2: off // 2 + n * esz // 2]
        if dtype != BF16:
            ap = ap.bitcast(dtype)
        if len(shape) == 3:
            ap = ap.rearrange("p (a b) -> p a b", b=shape[2])
        elif len(shape) == 4:
            ap = ap.rearrange("p (a b c) -> p a b c", b=shape[2], c=shape[3])
        return ap

    HT = aview(0, [128, KC, S])
    OBF = aview(32768, [128, KC, S])
    OBT = aview(32768, [128, NT, D])
    QT = [aview(65536 + i * 8192, [128, S]) for i in range(2)]
    KT = [aview(65536 + i * 8192 + 4096, [128, S]) for i in range(2)]
    VP0 = [aview(81920 + i * 4160, [128, NT, 2, 65]) for i in range(2)]
    VP1 = [aview(81920 + i * 4160, [128, NT, 128]) for i in range(2)]
    TB = 90240
    PT = [aview(TB + i * 1280, [128, 640]) for i in range(6)]
    ZB = [aview(TB + 7680 + i * 1024, [128, 256], F32) for i in range(2)]
    BT = [aview(TB + 9728 + i * 1024, [128, 256], F32) for i in range(2)]
    KPAD = [[aview(TB + 11776 + (h_ * 4 + r_) * 256, [128, 128]) for r_ in range(4)] for h_ in range(2)]
    E1 = [aview(TB + p * 2048, [128, 2, 512]) for p in range(3)]
    L1 = [aview(TB + 6144 + p * 2048, [128, 2, 512]) for p in range(2)]
    EX2 = aview(TB + 10240, [128, 2, 512])
    A1 = [aview(TB + 12288 + p * 2048, [128, 2, 512]) for p in range(2)]
    H2 = aview(0, [128, KC, 1024])
    ACTT = aview(16384, [128, FC, 1024])
    WD = aview(61440, [128, FC, D])

    PS = nc.alloc_psum_tensor("ps", [128, 4096], F32)

    def bank(b, n=512, c0=0):
        return PS[:, b * 512 + c0: b * 512 + c0 + n]

    def bank_bf(b):
        return PS[:, b * 512:(b + 1) * 512].bitcast(BF16)

    pk = lambda b: ("ps", b)

    P.op("sp", lambda e: e.dma_start(out=CF[:], in_=consts_d.rearrange("c p n -> p c n")), writes=["cf"], dma_slot="c0")
    P.op("sp", lambda e: e.dma_start(out=CB[:], in_=cb_d), writes=["cb"], dma_slot="c1")
    P.op("dve", lambda e: e.tensor_copy(out=IDB[:], in_=CF[:, 0, :]), reads=["cf"], writes=["idb"])
    P.op("dve", lambda e: e.tensor_copy(out=TRI[:], in_=CF[:, 1, :]), reads=["cf"], writes=["tri"])
    P.op("dve", lambda e: e.tensor_copy(out=TRIC[:], in_=CF[:, 2, :]), reads=["cf"], writes=["tric"])

    state = {"slab": 0, "g": 0, "ss": 0, "xs": 0, "tmp": 0, "nb": 0, "fence": 0}
    arena_keys = set()

    def akey(k):
        arena_keys.add(k)
        return k

    def fence():
        state["fence"] += 1
        fk = ("fence", state["fence"])
        keys = list(arena_keys)
        P.op("pool", lambda e: e.memset(ST[:, 15:16], 0.0), reads=[], writes=keys + [fk])
        arena_keys.clear()
        return fk

    cur_fence = [None]

    def fr():
        return [cur_fence[0]] if cur_fence[0] is not None else []

    def MM(out, lhsT, rhs, start, stop, reads, writes, skip=False):
        return P.op("pe", lambda e: e.matmul(out, lhsT=lhsT, rhs=rhs, start=start, stop=stop, skip_group_check=skip),
                    reads, writes)

    def TR(out, in_, reads, writes):
        return P.op("pe", lambda e: e.transpose(out=out, in_=in_, identity=IDB[:]), list(reads) + ["idb"], writes)

    def ACT(out, in_, func, reads, writes, **kw):
        return P.op("act", lambda e: e.activation(out=out, in_=in_, func=func, **kw), reads, writes)

    def TT(out, in0, in1, op, reads, writes):
        return P.op("dve", lambda e: e.tensor_tensor(out=out, in0=in0, in1=in1, op=op), reads, writes)

    def TS(out, in0, s1, op0, reads, writes):
        return P.op("dve", lambda e: e.tensor_scalar(out=out, in0=in0, scalar1=s1, scalar2=None, op0=op0), reads, writes)

    def STT(out, in0, scalar, in1, op0, op1, reads, writes):
        return P.op("dve", lambda e: e.scalar_tensor_tensor(out=out, in0=in0, scalar=scalar, in1=in1, op0=op0, op1=op1),
                    reads, writes)

    def CP(out, in_, reads, writes):
        return P.op("dve", lambda e: e.tensor_copy(out=out, in_=in_), reads, writes)

    def MS(ap, val, reads, writes):
        return P.op("dve", lambda e: e.memset(ap, val), reads, writes)

    def DMA(eng, out, in_, reads, writes, slot):
        return P.op(eng, lambda e: e.dma_start(out=out, in_=in_), reads, writes, dma_slot=slot)

    def load_slab(src_ap):
        r = state["slab"] % 6
        state["slab"] += 1
        DMA("pool", SLAB[r][:], src_ap.rearrange("(kc p) n -> p kc n", p=128), [], [("slab", r)], ("slab", r))
        return r

    def load_gain(idx):
        g = state["g"] % 2
        state["g"] += 1
        DMA("sp", G[g][:], gains_d[idx], [], [("G", g)], ("G", g))
        return g

    def rstd_ops(src_ap, src_keys):
        c = state["ss"] % 8
        state["ss"] += 1
        ss = ST[:, c:c + 1]
        k = ("ss", c)
        ACT(JUNK[:], src_ap, AF.Square, src_keys, [k], accum_out=ss)
        ACT(ss, ss, AF.Ln, [k], [k], scale=1.0 / D, bias=EPS)
        ACT(ss, ss, AF.Exp, [k], [k], scale=-0.5)
        return ss, k

    def norm_a(i, g):
        rs, rk = rstd_ops(X[:, i, :], [("X", i)])
        b = state["xs"] % 2
        state["xs"] += 1
        xs = XS[b]
        STT(xs[:], X[:, i, :], rs, G[g][:], ALU.mult, ALU.mult, [("X", i), rk, ("G", g)], [("xs", b)])
        pb = state["nb"] % 2
        state["nb"] += 1
        pbf = bank_bf(pb)
        for kc in range(KC):
            TR(pbf[:, kc * 128:(kc + 1) * 128], xs[:, kc * 128:(kc + 1) * 128], [("xs", b)], [pk(pb)])
        return pb

    def norm_b(pb, dst, dcol, dkey):
        ACT(dst[:, :, dcol:dcol + 128], bank_bf(pb).rearrange("p (a b) -> p a b", b=128), AF.Copy, [pk(pb)] + fr(), [akey(dkey)])

    def norm_tile(i, g, dst, dcol, dkey):
        norm_b(norm_a(i, g), dst, dcol, dkey)

    def norm_tiles(tiles, g, dst, dcol_of, dkey_of):
        prev = None
        for i in tiles:
            pb = norm_a(i, g)
            if prev is not None:
                norm_b(prev[0], dst, dcol_of(prev[1]), dkey_of(prev[1]))
            prev = (pb, i)
        norm_b(prev[0], dst, dcol_of(prev[1]), dkey_of(prev[1]))

    def post_norm_residual(i, b0, g):
        src = PS[:, b0 * 512: b0 * 512 + 1024]
        rs, rk = rstd_ops(src, [pk(b0), pk(b0 + 1)])
        for hf in range(2):
            t = state["tmp"] % 2
            state["tmp"] += 1
            xa = X[:, i, hf * 512:(hf + 1) * 512]
            STT(TMP[t][:], bank(b0 + hf), rs, G[g][:, hf * 512:(hf + 1) * 512], ALU.mult, ALU.mult,
                [pk(b0 + hf), rk, ("G", g)], [("tmp", t)])
            TT(xa, xa, TMP[t][:], ALU.add, [("tmp", t), ("X", i)], [("X", i)])

    def proj_pair(l, hp, slabs, buf, layer0):
        sq, sk, sv = slabs
        lo = (hp % 2) * 128
        for (sl, dstT, scale, nm) in ((sq, QT[buf], 0.125, "QT"), (sk, KT[buf], None, "KT")):
            for tb in range(4):
                pb = state["nb"] % 2
                state["nb"] += 1
                for kc in range(KC):
                    MM(bank(pb), SLAB[sl][:, kc, lo:lo + 128], HT[:, kc, tb * 512:(tb + 1) * 512], kc == 0, kc == KC - 1,
                       [("slab", sl), ("HT", tb)], [pk(pb)])
                    if kc < KC - 1:
                        yield
                if scale is not None:
                    TS(dstT[:, tb * 512:(tb + 1) * 512], bank(pb), scale, ALU.mult, [pk(pb)] + fr(), [akey((nm, buf, tb))])
                else:
                    CP(dstT[:, tb * 512:(tb + 1) * 512], bank(pb), [pk(pb)] + fr(), [akey((nm, buf, tb))])
                yield
        for t4 in range(4):
            pb = state["nb"] % 2
            state["nb"] += 1
            for j in range(4):
                i = t4 * 4 + j
                for kc in range(KC):
                    MM(bank(pb, 128, j * 128), HT[:, kc, i * 128:(i + 1) * 128], SLAB[sv][:, kc, lo:lo + 128],
                       kc == 0, kc == KC - 1, [("slab", sv), ("HT", i // 4)], [pk(pb)])
                    if kc % 2 == 1 and not (j == 3 and kc == KC - 1):
                        yield
            if layer0:
                CP(VP0[buf][:, t4 * 4:(t4 + 1) * 4, :, 0:64], bank(pb).rearrange("p (a b c) -> p a b c", b=2, c=64),
                   [pk(pb)] + fr(), [akey(("VP", buf, t4))])
            else:
                CP(VP1[buf][:, t4 * 4:(t4 + 1) * 4, :], bank(pb).rearrange("p (a b) -> p a b", b=128),
                   [pk(pb)] + fr(), [akey(("VP", buf, t4))])
            yield

    def attn0_pair(hp, buf):
        for hh in range(2):
            h = hp * 2 + hh
            DMA("sp", BT[h % 2], bias_d[h], fr(), [akey(("BT", h % 2))], ("BT", h % 2))
        NG = 2 * NT

        def s1(g_):
            hh, kb = divmod(g_, NT)
            h = hp * 2 + hh
            bt = h % 2
            q0 = kb * 128
            nq = min(640, S - q0)
            zb = 2 + 2 * (g_ % 2)
            pt = PT[g_ % 6]
            ptk = akey(("PT", g_ % 6))
            kp = KPAD[hh][kb % 4]
            kpk = ("KPAD", hh, kb % 4)
            nb_ = min(256, nq)
            MM(bank(zb, nb_), kp[:], QT[buf][:, q0:q0 + nb_], True, True,
               [kpk] + [("QT", buf, t) for t in range(q0 // 512, (q0 + nb_ - 1) // 512 + 1)], [pk(zb)])
            if nq > 256:
                MM(bank(zb + 1, nq - 256), kp[:], QT[buf][:, q0 + 256:q0 + nq], True, True,
                   [kpk] + [("QT", buf, t) for t in range((q0 + 256) // 512, (q0 + nq - 1) // 512 + 1)],
                   [pk(zb + 1)])
            z2 = g_ % 2
            if nq > 256:
                ACT(pt[:, 256:nq], bank(zb + 1, nq - 256), AF.Exp, [pk(zb + 1), "cb"] + fr(), [ptk], bias=CB[:, h:h + 1])
            if nq > 576:
                MS(pt[0:64, 576:640], 0.0, [], [ptk])
            TT(ZB[z2][:, 0:nb_], bank(zb, nb_), BT[bt][:, 0:nb_], ALU.add, [pk(zb), ("BT", bt)] + fr(), [akey(("ZB", z2))])
            ACT(pt[:, 0:nb_], ZB[z2][:, 0:nb_], AF.Exp, [("ZB", z2)], [ptk])

        def s2(g_):
            hh, kb = divmod(g_, NT)
            h = hp * 2 + hh
            pvb = 6 + (g_ % 2)
            k0 = max(0, kb - 4)
            for kk in range(k0, kb + 1):
                sl_ = (hh * NT + kk) % 6
                MM(bank(pvb, 65), PT[sl_][:, (kb - kk) * 128:(kb - kk + 1) * 128], VP0[buf][:, kk, hh, :],
                   kk == k0, kk == kb, [("PT", sl_), ("VP", buf, kk // 4), ("VPones", buf)], [pk(pvb)])
            c = state["ss"] % 8
            state["ss"] += 1
            rc = ST[:, c:c + 1]
            src = bank(pvb, 1, 64)
            P.op("dve", lambda e: e.reciprocal(out=rc, in_=src), [pk(pvb)], [("ss", c)])
            TS(OBT[:, kb, h * 64:(h + 1) * 64], bank(pvb, 64), rc, ALU.mult, [pk(pvb), ("ss", c)] + fr(), [akey(("OBT", kb))])

        def kcopy(g_):
            hh, kb = divmod(g_, NT)
            pr = slice(hh * 64, hh * 64 + 64)
            CP(KPAD[hh][kb % 4][pr, :], KT[buf][pr, kb * 128:(kb + 1) * 128],
               [("KT", buf, kb // 4), ("KPADZ", hh)] + fr(), [akey(("KPAD", hh, kb % 4))])

        kcopy(0)
        kcopy(1)
        s1(0)
        for g_ in range(NG):
            if g_ + 2 < NG:
                kcopy(g_ + 2)
            if g_ + 1 < NG:
                s1(g_ + 1)
            yield
            s2(g_)

    def attn1_helpers(hp, buf):
        ZV = PS[:, 1024:2048].rearrange("p (h c) -> p h c", h=2)
        P2V = PS[:, 2048:3072].rearrange("p (h c) -> p h c", h=2)

        def cz(qb0, kb):
            return max(0, kb - qb0) * 128

        def z_mm(qb0, kb):
            c0 = cz(qb0, kb)
            for hh in range(2):
                pr = slice(hh * 64, hh * 64 + 64)
                MM(bank(2 + hh, 512 - c0, c0), KT[buf][pr, kb * 128:(kb + 1) * 128],
                   QT[buf][pr, qb0 * 128 + c0:qb0 * 128 + 512], True, True,
                   [("KT", buf, kb // 4), ("QT", buf, qb0 // 4)], [pk(2 + hh)])

        def exp_e(qb0, kb, par):
            c0 = cz(qb0, kb)
            ek = akey(("E1", par))
            ACT(E1[par][:, :, c0:512], ZV[:, :, c0:512], AF.Exp, [pk(2), pk(3)] + fr(), [ek])
            if kb >= qb0:
                TT(E1[par][:, :, c0:c0 + 128], E1[par][:, :, c0:c0 + 128], MASK2, ALU.mult, [ek, "cf"], [ek])

        def ln_l(qb0, kb, pe_, par):
            c0 = cz(qb0, kb)
            ACT(L1[par][:, :, c0:512], E1[pe_][:, :, c0:512], AF.Ln, [("E1", pe_)] + fr(), [akey(("L1", par))], bias=1.0)

        def tri(qb0, kb, par, first, last):
            c0 = cz(qb0, kb)
            for hh in range(2):
                MM(bank(4 + hh, 512 - c0, c0), TRI[:], L1[par][:, hh, c0:512], first, last,
                   [("L1", par), "tri"], [pk(4 + hh)], skip=True)

        def ex2(qb0, kb):
            c0 = cz(qb0, kb)
            ACT(EX2[:, :, c0:512], P2V[:, :, c0:512], AF.Exp, [pk(4), pk(5)] + fr(), [akey("EX2")], scale=-1.0)

        def tric(qb0, kb, par, prelast):
            c0 = cz(qb0, kb)
            for hh in range(2):
                MM(bank(4 + hh, 512 - c0, c0), TRIC[:], L1[par][:, hh, c0:512], False, prelast,
                   [("L1", par), "tric"], [pk(4 + hh)], skip=True)

        def a_prod(qb0, kb, pe_, par):
            c0 = cz(qb0, kb)
            TT(A1[par][:, :, c0:512], E1[pe_][:, :, c0:512], EX2[:, :, c0:512], ALU.mult,
               [("E1", pe_), "EX2"] + fr(), [akey(("A1", par))])

        def av(qb0, kb, par, first, last):
            c0 = cz(qb0, kb)
            for hh in range(2):
                MM(bank(6 + hh, 512 - c0, c0), VP1[buf][:, kb, :], A1[par][:, hh, c0:512], first, last,
                   [("A1", par), ("VP", buf, kb // 4)], [pk(6 + hh)], skip=True)

        def evac(qb0):
            for hh in range(2):
                pr = slice(hh * 64, hh * 64 + 64)
                CP(OBF[pr, hp, qb0 * 128:qb0 * 128 + 512], PS[pr, (6 + hh) * 512:(7 + hh) * 512],
                   [pk(6 + hh)] + fr(), [akey(("OBF", hp, qb0 // 4))])

        return dict(z_mm=z_mm, exp_e=exp_e, ln_l=ln_l, tri=tri, ex2=ex2, tric=tric, a_prod=a_prod, av=av, evac=evac)

    def attn1_all(before_pair):
        hs = {}

        def H(hp):
            if hp not in hs:
                before_pair(hp)
                hs[hp] = attn1_helpers(hp, hp % 2)
            return hs[hp]

        t = [(hp, qb0, qb0 + 3 - k, k, qb0 + 4) for hp in range(8) for qb0 in range(0, NT, 4) for k in range(qb0 + 4)]
        ng = len(t)
        H(t[0][0])["z_mm"](t[0][1], t[0][2])
        H(t[0][0])["exp_e"](t[0][1], t[0][2], 0)
        H(t[1][0])["z_mm"](t[1][1], t[1][2])
        for g_ in range(ng):
            hp, qb0, kb, k, ns = t[g_]
            h = H(hp)
            h["ln_l"](qb0, kb, g_ % 3, g_ % 2)
            h["tri"](qb0, kb, g_ % 2, k == 0, k == ns - 1)
            if g_ + 1 < ng:
                n1 = t[g_ + 1]
                H(n1[0])["exp_e"](n1[1], n1[2], (g_ + 1) % 3)
            h["ex2"](qb0, kb)
            if k >= 1:
                h["av"](qb0, t[g_ - 1][2], (g_ - 1) % 2, k - 1 == 0, False)
            if g_ + 2 < ng:
                n2 = t[g_ + 2]
                H(n2[0])["z_mm"](n2[1], n2[2])
            yield hp
            if k < ns - 1:
                h["tric"](qb0, kb, g_ % 2, k == ns - 2)
            h["a_prod"](qb0, kb, g_ % 3, g_ % 2)
            if k == ns - 1:
                h["av"](qb0, kb, g_ % 2, False, True)
                h["evac"](qb0)

    def mixer(l):
        layer0 = (l % 2 == 0)
        g = load_gain(l * 4 + 0)
        norm_tiles(range(NT), g, HT, lambda i: i * 128, lambda i: ("HT", i // 4))
        if layer0:
            for b_ in range(2):
                MS(VP0[b_][:, :, :, 64:65], 1.0, fr(), [akey(("VPones", b_))])
            for h_ in range(2):
                for r in range(4):
                    MS(KPAD[h_][r][(1 - h_) * 64:(2 - h_) * 64, :], 0.0, fr(), [akey(("KPADZ", h_))])
        slabs = {}

        def get_slabs(hp):
            gq = hp // 2
            if gq not in slabs:
                slabs[gq] = tuple(load_slab(wqkv_d[l][:, part * D + gq * 256: part * D + gq * 256 + 256]) for part in range(3))
            return slabs[gq]

        get_slabs(0)
        get_slabs(2)
        for _ in proj_pair(l, 0, get_slabs(0), 0, layer0):
            pass
        if layer0:
            for hp in range(8):
                nxt = proj_pair(l, hp + 1, get_slabs(hp + 1), (hp + 1) % 2, layer0) if hp + 1 < 8 else None
                interleave(attn0_pair(hp, hp % 2), nxt, 4)
                if hp % 2 == 1 and hp + 3 < 8:
                    get_slabs(hp + 3)
        else:
            projgens = {}

            def before_pair(hp):
                gen = projgens.get(hp)
                if gen is not None:
                    for _ in gen:
                        pass

            cur = -1
            for hp_t in attn1_all(before_pair):
                if hp_t != cur:
                    cur = hp_t
                    p_ = hp_t - 1
                    if p_ >= 0 and p_ % 2 == 1 and p_ + 3 < 8:
                        get_slabs(p_ + 3)
                    if hp_t + 1 < 8:
                        projgens[hp_t + 1] = proj_pair(l, hp_t + 1, get_slabs(hp_t + 1), (hp_t + 1) % 2, layer0)
                gen = projgens.get(hp_t + 1)
                if gen is not None:
                    for _i in range(4):
                        next(gen, None)
        wos = [load_slab(wo_d[l][:, c4 * 256:(c4 + 1) * 256]) for c4 in range(4)]
        g = load_gain(l * 4 + 1)

        def wo_tile(i, OT, okeys):
            b0 = 2 + 2 * (i % 3)
            for c4 in range(4):
                for kc in range(KC):
                    MM(bank(b0 + c4 // 2, 256, (c4 % 2) * 256), OT[:, kc, i * 128:(i + 1) * 128], SLAB[wos[c4]][:, kc, :],
                       kc == 0, kc == KC - 1, okeys + [("slab", wos[c4])], [pk(b0 + c4 // 2)])
            post_norm_residual(i, b0, g)

        if layer0:
            def tr_tile(i):
                pb = state["nb"] % 2
                state["nb"] += 1
                pbf = bank_bf(pb)
                for kc in range(KC):
                    TR(pbf[:, kc * 128:(kc + 1) * 128], OBT[:, i, kc * 128:(kc + 1) * 128], [("OBT", i)], [pk(pb)])
                ACT(HT[:, :, i * 128:(i + 1) * 128], pbf.rearrange("p (a b) -> p a b", b=128), AF.Copy,
                    [pk(pb)] + fr(), [akey(("HT", i // 4)), akey(("OT0", i))])

            tr_tile(0)
            for i in range(NT):
                if i + 1 < NT:
                    tr_tile(i + 1)
                wo_tile(i, HT, [("OT0", i)])
        else:
            for i in range(NT):
                wo_tile(i, OBF, [("OBF", hp_, i // 4) for hp_ in range(8)])

    def ffn(l, tile_hook=None):
        cur_fence[0] = fence()
        for j in range(6):
            k0 = j * 4
            k1 = min(FC, k0 + 4)
            DMA("pool", WD[:, k0:k1, :], wd_d[l][k0 * 128:k1 * 128, :].rearrange("(kc p) n -> p kc n", p=128),
                fr(), [akey(("WD", j))], ("WD", j))
        g = load_gain(l * 4 + 2)
        g2 = load_gain(l * 4 + 3)
        norm_tiles(range(8), g, H2, lambda i: i * 128, lambda i: ("H2", i // 4))
        for half in range(2):
            for sp_ in range(11):
                sg = load_slab(wg_d[l][:, sp_ * 256:(sp_ + 1) * 256])
                su = load_slab(wu_d[l][:, sp_ * 256:(sp_ + 1) * 256])
                for f2 in range(2):
                    fb = sp_ * 2 + f2
                    for tb in range(2):
                        gb = 0 + 2 * tb
                        ub = 1 + 2 * tb
                        for (sl, bb) in ((sg, gb), (su, ub)):
                            for kc in range(KC):
                                MM(bank(bb), SLAB[sl][:, kc, f2 * 128:(f2 + 1) * 128], H2[:, kc, tb * 512:(tb + 1) * 512],
                                   kc == 0, kc == KC - 1, [("slab", sl), ("H2", tb)], [pk(bb)])
                        t = state["tmp"] % 2
                        state["tmp"] += 1
                        ACT(TMP[t][:], bank(gb), AF.Silu, [pk(gb)], [("tmp", t)])
                        TT(ACTT[:, fb, tb * 512:(tb + 1) * 512], bank(ub), TMP[t][:], ALU.mult,
                           [pk(ub), ("tmp", t)] + fr(), [akey(("ACTT", fb, tb))])
            for il in range(8):
                i = half * 8 + il
                b0 = 4 + 2 * (il % 2)
                for kc in range(FC):
                    for hf in range(2):
                        MM(bank(b0 + hf), ACTT[:, kc, il * 128:(il + 1) * 128], WD[:, kc, hf * 512:(hf + 1) * 512],
                           kc == 0, kc == FC - 1, [("ACTT", kc, il // 4), ("WD", kc // 4)], [pk(b0 + hf)])
                if half == 0:
                    norm_tile(8 + il, g, H2, il * 128, ("H2", il // 4))
                post_norm_residual(i, b0, g2)
                if tile_hook is not None:
                    tile_hook(i)
        cur_fence[0] = fence()

    final = []
    xrow = lambda ap, i0, n: ap[i0 * 128:(i0 + n) * 128, :].rearrange("(i p) d -> p i d", p=128)
    for i in range(NT):
        DMA("sp", X[:, i:i + 1, :], xrow(x_d[0], i, 1), [], [("X", i)], ("x0", i))
    for s in range(nseq):
        last_l = layers[-1]

        def hook(i, s=s):
            if i % 4 != 3:
                return
            j = i // 4
            keys = [("X", j * 4 + t) for t in range(4)]
            o = DMA("sp", xrow(y_d[s], j * 4, 4), X[:, j * 4:(j + 1) * 4, :], keys, [], ("y", j))
            final.append(o)
            if s + 1 < nseq:
                DMA("sp", X[:, j * 4:(j + 1) * 4, :], xrow(x_d[s + 1], j * 4, 4), [], keys, ("x", j))

        for l in layers:
            if do_mix:
                mixer(l)
            if do_ffn:
                ffn(l, hook if l == last_l else None)
        if not do_ffn:
            for i in range(NT):
                hook(i)
    stats = P.emit(final_waits=final[-4:] if len(final) >= 4 else final)
    return nc, stats


def host_consts(rel_bias):
    j = np.arange(128)[:, None]
    s = np.arange(128)[None, :]
    consts = np.zeros((5, 128, 128), np.float32)
    consts[0] = np.eye(128, dtype=np.float32)
    consts[1] = (j >= s).astype(np.float32)
    consts[2] = (j < s).astype(np.float32)
    consts[3] = (j < s).astype(np.float32)
    consts[4] = consts[3]
    sl = np.arange(128)[:, None]
    tl = np.arange(256)[None, :]
    idx = np.clip(tl - sl, -63, 128) + 63
    rb = np.asarray(rel_bias, np.float32)[0]
    tiles = rb[:, idx]
    invisible = (sl >= 64) & (tl < 64)
    tiles = np.where(invisible[None], np.float32(NEG), tiles).astype(np.float32)
    cb = np.broadcast_to(rb[:, 191][None, :], (128, NH)).astype(np.float32).copy()
    return consts, np.ascontiguousarray(tiles), cb


def host_gains(g_pre_mix, g_post_mix, g_pre_ffn, g_post_ffn):
    out = np.zeros((8, 128, D), np.float32)
    for l in range(2):
        for k, g in enumerate((g_pre_mix, g_post_mix, g_pre_ffn, g_post_ffn)):
            out[l * 4 + k] = np.broadcast_to(np.asarray(g, np.float32)[l][None, :], (128, D))
    return out


_CACHE = {}


def kernel(x, g_pre_mix, g_post_mix, w_qkv, w_o, rel_bias, g_pre_ffn, g_post_ffn, w_gate, w_up, w_down):
    n = 8
    x = np.asarray(x, np.float32)
    if "nc" not in _CACHE:
        _CACHE["nc"] = build_program(nseq=2)[0]
    nc = _CACHE["nc"]
    consts, tiles, cb = host_consts(rel_bias)
    gains = host_gains(g_pre_mix, g_post_mix, g_pre_ffn, g_post_ffn)
    shared = {
        "w_qkv": np.ascontiguousarray(np.asarray(w_qkv, np.float32)),
        "w_o": np.ascontiguousarray(np.asarray(w_o, np.float32)),
        "w_gate": np.ascontiguousarray(np.asarray(w_gate, np.float32)),
        "w_up": np.ascontiguousarray(np.asarray(w_up, np.float32)),
        "w_down": np.ascontiguousarray(np.asarray(w_down, np.float32)),
        "gains": gains, "bias_t": tiles, "cbias": cb, "consts": consts,
    }
    in_maps = []
    for c in range(n):
        m = dict(shared)
        m["x"] = np.ascontiguousarray(x[2 * c:2 * c + 2])
        in_maps.append(m)
    res = run_bass_kernel_spmd(nc, in_maps, core_ids=list(range(n)))
    out = np.concatenate([np.asarray(r["y"], np.float32) for r in res.results], axis=0)
    return out
```

```python
import numpy as np
import concourse.bass as bass
import concourse.mybir as mybir
from concourse.bass_utils import run_bass_kernel_spmd

F32 = mybir.dt.float32
BF16 = mybir.dt.bfloat16
AF = mybir.ActivationFunctionType
ALU = mybir.AluOpType

D = 1024
S = 2048
NH = 16
DFF = 2816
NT = 16
KC = 8
FC = 22
EPS = 1e-6
NEG = -30000.0

ENGS = ("pe", "act", "dve", "pool", "sp")


class Op:
    __slots__ = ("eng", "fn", "deps", "is_dma", "dma_sem", "dma_val", "sig", "idx")

    def __init__(self, eng, fn, deps, is_dma):
        self.eng = eng
        self.fn = fn
        self.deps = deps
        self.is_dma = is_dma
        self.dma_sem = None
        self.dma_val = 0
        self.sig = 0
        self.idx = -1


class Prog:
    def __init__(self, nc):
        self.nc = nc
        self.ops = []
        self.per_eng = {e: [] for e in ENGS}
        self.last_w = {}
        self.readers = {}
        self.dma_count = {}
        self.same_eng_window = 6
        self._n = 0

    def op(self, eng, fn, reads=(), writes=(), dma_slot=None):
        psr = [k for k in reads if isinstance(k, tuple) and k[0] == "ps"]
        if psr:
            reads = [k for k in reads if not (isinstance(k, tuple) and k[0] == "ps")]
            writes = list(writes) + [k for k in psr if k not in writes]
        deps = set()
        for k in reads:
            w = self.last_w.get(k)
            if w is not None:
                deps.add(w)
        for k in writes:
            w = self.last_w.get(k)
            if w is not None:
                deps.add(w)
            rd = self.readers.get(k)
            if rd:
                deps.update(rd.values())
        is_dma = dma_slot is not None
        o = Op(eng, fn, deps, is_dma)
        if is_dma:
            c = self.dma_count.get(dma_slot, 0) + 16
            self.dma_count[dma_slot] = c
            o.dma_sem = dma_slot
            o.dma_val = c
        o.idx = len(self.per_eng[eng])
        self.per_eng[eng].append(o)
        self.ops.append(o)
        for k in writes:
            self.last_w[k] = o
            self.readers[k] = {}
        self._n += 1
        for k in reads:
            rd = self.readers.setdefault(k, {})
            rd[(eng, self._n) if is_dma else eng] = o
        return o

    def emit(self, final_waits=()):
        nc = self.nc
        needed = set()
        for o in self.ops:
            eff = []
            for d in o.deps:
                if not d.is_dma and d.eng == o.eng and not o.is_dma:
                    if o.eng == "pe":
                        continue
                    if o.idx - d.idx > self.same_eng_window:
                        continue
                eff.append(d)
                if not d.is_dma:
                    needed.add(d)
            o.deps = eff
        for o in final_waits:
            if not o.is_dma:
                needed.add(o)
        for e in ENGS:
            c = 0
            for o in self.per_eng[e]:
                if o in needed:
                    c += 1
                    o.sig = c
        sems = {e: nc.alloc_semaphore("prog_" + e) for e in ENGS}
        dsems = {}
        for i, k in enumerate(self.dma_count):
            dsems[k] = nc.alloc_semaphore("dma%d" % i)
        engobj = {"pe": "tensor", "act": "scalar", "dve": "vector", "pool": "gpsimd", "sp": "sync"}
        per_eng = self.per_eng
        stats = {e: [len(per_eng[e]), 0] for e in ENGS}

        def make(e):
            def body(eng):
                waited = {}
                for o in per_eng[e]:
                    for d in o.deps:
                        if d.is_dma:
                            key = ("d", d.dma_sem)
                            if waited.get(key, 0) < d.dma_val:
                                eng.wait_ge(dsems[d.dma_sem], d.dma_val)
                                waited[key] = d.dma_val
                                stats[e][1] += 1
                        else:
                            key = ("e", d.eng)
                            if waited.get(key, 0) < d.sig:
                                eng.wait_ge(sems[d.eng], d.sig)
                                waited[key] = d.sig
                                stats[e][1] += 1
                    ins = o.fn(eng)
                    if o.is_dma:
                        ins.then_inc(dsems[o.dma_sem], 16)
                    elif o.sig:
                        ins.then_inc(sems[e], 1)
                if e == "sp":
                    for o in final_waits:
                        if o.is_dma:
                            eng.wait_ge(dsems[o.dma_sem], o.dma_val)
                        else:
                            eng.wait_ge(sems[o.eng], o.sig)
            return body

        with nc.Block() as block:
            for e in ENGS:
                if per_eng[e] or e == "sp":
                    getattr(block, engobj[e])(make(e))
        self.stats = stats
        return stats


def interleave(main, side, n_side):
    side_done = side is None
    for _ in main:
        if not side_done:
            for _i in range(n_side):
                try:
                    next(side)
                except StopIteration:
                    side_done = True
                    break
    if not side_done:
        for _ in side:
            pass


def build_program(nseq=2, layers=(0, 1), do_ffn=True, do_mix=True):
    nc = bass.Bass("TRN2", target_bir_lowering=False, dynamic_dma_scratch_size=4096)
    dt = lambda name, shape, kind="ExternalInput": nc.dram_tensor(name, shape, F32, kind=kind).ap()
    x_d = dt("x", [nseq, S, D])
    y_d = dt("y", [nseq, S, D], kind="ExternalOutput")
    wqkv_d = dt("w_qkv", [2, D, 3 * D])
    wo_d = dt("w_o", [2, D, D])
    wg_d = dt("w_gate", [2, D, DFF])
    wu_d = dt("w_up", [2, D, DFF])
    wd_d = dt("w_down", [2, DFF, D])
    gains_d = dt("gains", [8, 128, D])
    bias_d = dt("bias_t", [NH, 128, 256])
    cb_d = dt("cbias", [128, NH])
    consts_d = dt("consts", [5, 128, 128])

    P = Prog(nc)

    X = nc.alloc_sbuf_tensor("X", [128, NT, D], F32)
    ARENA_B = 106624
    ARENA = nc.alloc_sbuf_tensor("ARENA", [128, ARENA_B // 2], BF16)
    SLAB = [nc.alloc_sbuf_tensor("slab%d" % i, [128, KC, 256], BF16) for i in range(6)]
    G = [nc.alloc_sbuf_tensor("G%d" % i, [128, D], F32) for i in range(2)]
    TMP = [nc.alloc_sbuf_tensor("tmp%d" % i, [128, 512], F32) for i in range(2)]
    JUNK = nc.alloc_sbuf_tensor("junk", [128, D], BF16)
    XS = [nc.alloc_sbuf_tensor("xs%d" % i, [128, D], BF16) for i in range(2)]
    CF = nc.alloc_sbuf_tensor("cf", [128, 5, 128], F32)
    IDB = nc.alloc_sbuf_tensor("idb", [128, 128], BF16)
    TRI = nc.alloc_sbuf_tensor("tri", [128, 128], BF16)
    TRIC = nc.alloc_sbuf_tensor("tric", [128, 128], BF16)
    CB = nc.alloc_sbuf_tensor("cb", [128, NH], F32)
    ST = nc.alloc_sbuf_tensor("st", [128, 16], F32)
    MASK2 = CF[:, 3:5, :]

    def aview(off, shape, dtype=BF16):
        n = 1
        for s_ in shape[1:]:
            n *= s_
        esz = 2 if dtype == BF16 else 4
        ap = ARENA[:, off // 2: off // 2 + n * esz // 2]
        if dtype != BF16:
            ap = ap.bitcast(dtype)
        if len(shape) == 3:
            ap = ap.rearrange("p (a b) -> p a b", b=shape[2])
        elif len(shape) == 4:
            ap = ap.rearrange("p (a b c) -> p a b c", b=shape[2], c=shape[3])
        return ap

    HT = aview(0, [128, KC, S])
    OBF = aview(32768, [128, KC, S])
    OBT = aview(32768, [128, NT, D])
    QT = [aview(65536 + i * 8192, [128, S]) for i in range(2)]
    KT = [aview(65536 + i * 8192 + 4096, [128, S]) for i in range(2)]
    VP0 = [aview(81920 + i * 4160, [128, NT, 2, 65]) for i in range(2)]
    VP1 = [aview(81920 + i * 4160, [128, NT, 128]) for i in range(2)]
    TB = 90240
    PT = [aview(TB + i * 1280, [128, 640]) for i in range(6)]
    ZB = [aview(TB + 7680 + i * 1024, [128, 256], F32) for i in range(2)]
    BT = [aview(TB + 9728 + i * 1024, [128, 256], F32) for i in range(2)]
    KPAD = [[aview(TB + 11776 + (h_ * 4 + r_) * 256, [128, 128]) for r_ in range(4)] for h_ in range(2)]
    E1 = [aview(TB + p * 2048, [128, 2, 512]) for p in range(3)]
    L1 = [aview(TB + 6144 + p * 2048, [128, 2, 512]) for p in range(2)]
    EX2 = aview(TB + 10240, [128, 2, 512])
    A1 = [aview(TB + 12288 + p * 2048, [128, 2, 512]) for p in range(2)]
    H2 = aview(0, [128, KC, 1024])
    ACTT = aview(16384, [128, FC, 1024])
    WD = aview(61440, [128, FC, D])

    PS = nc.alloc_psum_tensor("ps", [128, 4096], F32)

    def bank(b, n=512, c0=0):
        return PS[:, b * 512 + c0: b * 512 + c0 + n]

    def bank_bf(b):
        return PS[:, b * 512:(b + 1) * 512].bitcast(BF16)

    pk = lambda b: ("ps", b)

    P.op("sp", lambda e: e.dma_start(out=CF[:], in_=consts_d.rearrange("c p n -> p c n")), writes=["cf"], dma_slot="c0")
    P.op("sp", lambda e: e.dma_start(out=CB[:], in_=cb_d), writes=["cb"], dma_slot="c1")
    P.op("dve", lambda e: e.tensor_copy(out=IDB[:], in_=CF[:, 0, :]), reads=["cf"], writes=["idb"])
    P.op("dve", lambda e: e.tensor_copy(out=TRI[:], in_=CF[:, 1, :]), reads=["cf"], writes=["tri"])
    P.op("dve", lambda e: e.tensor_copy(out=TRIC[:], in_=CF[:, 2, :]), reads=["cf"], writes=["tric"])

    state = {"slab": 0, "g": 0, "ss": 0, "xs": 0, "tmp": 0, "nb": 0, "fence": 0}
    arena_keys = set()

    def akey(k):
        arena_keys.add(k)
        return k

    def fence():
        state["fence"] += 1
        fk = ("fence", state["fence"])
        keys = list(arena_keys)
        P.op("dve", lambda e: e.memset(ST[:, 15:16], 0.0), reads=[], writes=keys + [fk])
        arena_keys.clear()
        return fk

    cur_fence = [None]

    def fr():
        return [cur_fence[0]] if cur_fence[0] is not None else []

    def MM(out, lhsT, rhs, start, stop, reads, writes, skip=False):
        return P.op("pe", lambda e: e.matmul(out, lhsT=lhsT, rhs=rhs, start=start, stop=stop, skip_group_check=skip),
                    reads, writes)

    def TR(out, in_, reads, writes):
        return P.op("pe", lambda e: e.transpose(out=out, in_=in_, identity=IDB[:]), list(reads) + ["idb"], writes)

    def ACT(out, in_, func, reads, writes, **kw):
        return P.op("act", lambda e: e.activation(out=out, in_=in_, func=func, **kw), reads, writes)

    def TT(out, in0, in1, op, reads, writes):
        return P.op("dve", lambda e: e.tensor_tensor(out=out, in0=in0, in1=in1, op=op), reads, writes)

    def TS(out, in0, s1, op0, reads, writes):
        return P.op("dve", lambda e: e.tensor_scalar(out=out, in0=in0, scalar1=s1, scalar2=None, op0=op0), reads, writes)

    def STT(out, in0, scalar, in1, op0, op1, reads, writes):
        return P.op("dve", lambda e: e.scalar_tensor_tensor(out=out, in0=in0, scalar=scalar, in1=in1, op0=op0, op1=op1),
                    reads, writes)

    def CP(out, in_, reads, writes):
        return P.op("dve", lambda e: e.tensor_copy(out=out, in_=in_), reads, writes)

    def MS(ap, val, reads, writes):
        return P.op("dve", lambda e: e.memset(ap, val), reads, writes)

    def DMA(eng, out, in_, reads, writes, slot):
        return P.op(eng, lambda e: e.dma_start(out=out, in_=in_), reads, writes, dma_slot=slot)

    def load_slab(src_ap):
        r = state["slab"] % 6
        state["slab"] += 1
        DMA("pool", SLAB[r][:], src_ap.rearrange("(kc p) n -> p kc n", p=128), [], [("slab", r)], ("slab", r))
        return r

    def load_gain(idx):
        g = state["g"] % 2
        state["g"] += 1
        DMA("sp", G[g][:], gains_d[idx], [], [("G", g)], ("G", g))
        return g

    def rstd_ops(src_ap, src_keys):
        c = state["ss"] % 8
        state["ss"] += 1
        ss = ST[:, c:c + 1]
        k = ("ss", c)
        ACT(JUNK[:], src_ap, AF.Square, src_keys, [k], accum_out=ss)
        ACT(ss, ss, AF.Ln, [k], [k], scale=1.0 / D, bias=EPS)
        ACT(ss, ss, AF.Exp, [k], [k], scale=-0.5)
        return ss, k

    def norm_a(i, g):
        rs, rk = rstd_ops(X[:, i, :], [("X", i)])
        b = state["xs"] % 2
        state["xs"] += 1
        xs = XS[b]
        STT(xs[:], X[:, i, :], rs, G[g][:], ALU.mult, ALU.mult, [("X", i), rk, ("G", g)], [("xs", b)])
        pb = state["nb"] % 2
        state["nb"] += 1
        pbf = bank_bf(pb)
        for kc in range(KC):
            TR(pbf[:, kc * 128:(kc + 1) * 128], xs[:, kc * 128:(kc + 1) * 128], [("xs", b)], [pk(pb)])
        return pb

    def norm_b(pb, dst, dcol, dkey):
        ACT(dst[:, :, dcol:dcol + 128], bank_bf(pb).rearrange("p (a b) -> p a b", b=128), AF.Copy, [pk(pb)] + fr(), [akey(dkey)])

    def norm_tile(i, g, dst, dcol, dkey):
        norm_b(norm_a(i, g), dst, dcol, dkey)

    def norm_tiles(tiles, g, dst, dcol_of, dkey_of):
        prev = None
        for i in tiles:
            pb = norm_a(i, g)
            if prev is not None:
                norm_b(prev[0], dst, dcol_of(prev[1]), dkey_of(prev[1]))
            prev = (pb, i)
        norm_b(prev[0], dst, dcol_of(prev[1]), dkey_of(prev[1]))

    def post_norm_residual(i, b0, g):
        src = PS[:, b0 * 512: b0 * 512 + 1024]
        rs, rk = rstd_ops(src, [pk(b0), pk(b0 + 1)])
        for hf in range(2):
            t = state["tmp"] % 2
            state["tmp"] += 1
            xa = X[:, i, hf * 512:(hf + 1) * 512]
            STT(TMP[t][:], bank(b0 + hf), rs, G[g][:, hf * 512:(hf + 1) * 512], ALU.mult, ALU.mult,
                [pk(b0 + hf), rk, ("G", g)], [("tmp", t)])
            TT(xa, xa, TMP[t][:], ALU.add, [("tmp", t), ("X", i)], [("X", i)])

    def proj_pair(l, hp, slabs, buf, layer0):
        sq, sk, sv = slabs
        lo = (hp % 2) * 128
        for (sl, dstT, scale, nm) in ((sq, QT[buf], 0.125, "QT"), (sk, KT[buf], None, "KT")):
            for tb in range(4):
                pb = state["nb"] % 2
                state["nb"] += 1
                for kc in range(KC):
                    MM(bank(pb), SLAB[sl][:, kc, lo:lo + 128], HT[:, kc, tb * 512:(tb + 1) * 512], kc == 0, kc == KC - 1,
                       [("slab", sl), ("HT", tb)], [pk(pb)])
                    if kc < KC - 1:
                        yield
                if scale is not None:
                    TS(dstT[:, tb * 512:(tb + 1) * 512], bank(pb), scale, ALU.mult, [pk(pb)] + fr(), [akey((nm, buf, tb))])
                else:
                    CP(dstT[:, tb * 512:(tb + 1) * 512], bank(pb), [pk(pb)] + fr(), [akey((nm, buf, tb))])
                yield
        for t4 in range(4):
            pb = state["nb"] % 2
            state["nb"] += 1
            for j in range(4):
                i = t4 * 4 + j
                for kc in range(KC):
                    MM(bank(pb, 128, j * 128), HT[:, kc, i * 128:(i + 1) * 128], SLAB[sv][:, kc, lo:lo + 128],
                       kc == 0, kc == KC - 1, [("slab", sv), ("HT", i // 4)], [pk(pb)])
                    if kc % 2 == 1 and not (j == 3 and kc == KC - 1):
                        yield
            if layer0:
                CP(VP0[buf][:, t4 * 4:(t4 + 1) * 4, :, 0:64], bank(pb).rearrange("p (a b c) -> p a b c", b=2, c=64),
                   [pk(pb)] + fr(), [akey(("VP", buf, t4))])
            else:
                CP(VP1[buf][:, t4 * 4:(t4 + 1) * 4, :], bank(pb).rearrange("p (a b) -> p a b", b=128),
                   [pk(pb)] + fr(), [akey(("VP", buf, t4))])
            yield

    def attn0_pair(hp, buf):
        for hh in range(2):
            h = hp * 2 + hh
            DMA("sp", BT[h % 2], bias_d[h], fr(), [akey(("BT", h % 2))], ("BT", h % 2))
        NG = 2 * NT

        def s1(g_):
            hh, kb = divmod(g_, NT)
            h = hp * 2 + hh
            bt = h % 2
            q0 = kb * 128
            nq = min(640, S - q0)
            zb = 2 + 2 * (g_ % 2)
            pt = PT[g_ % 6]
            ptk = akey(("PT", g_ % 6))
            kp = KPAD[hh][kb % 4]
            kpk = ("KPAD", hh, kb % 4)
            nb_ = min(256, nq)
            MM(bank(zb, nb_), kp[:], QT[buf][:, q0:q0 + nb_], True, True,
               [kpk] + [("QT", buf, t) for t in range(q0 // 512, (q0 + nb_ - 1) // 512 + 1)], [pk(zb)])
            if nq > 256:
                MM(bank(zb + 1, nq - 256), kp[:], QT[buf][:, q0 + 256:q0 + nq], True, True,
                   [kpk] + [("QT", buf, t) for t in range((q0 + 256) // 512, (q0 + nq - 1) // 512 + 1)],
                   [pk(zb + 1)])
            z2 = g_ % 2
            if nq > 256:
                ACT(pt[:, 256:nq], bank(zb + 1, nq - 256), AF.Exp, [pk(zb + 1), "cb"] + fr(), [ptk], bias=CB[:, h:h + 1])
            if nq > 576:
                MS(pt[0:64, 576:640], 0.0, [], [ptk])
            TT(ZB[z2][:, 0:nb_], bank(zb, nb_), BT[bt][:, 0:nb_], ALU.add, [pk(zb), ("BT", bt)] + fr(), [akey(("ZB", z2))])
            ACT(pt[:, 0:nb_], ZB[z2][:, 0:nb_], AF.Exp, [("ZB", z2)], [ptk])

        def s2(g_):
            hh, kb = divmod(g_, NT)
            h = hp * 2 + hh
            pvb = 6 + (g_ % 2)
            k0 = max(0, kb - 4)
            for kk in range(k0, kb + 1):
                sl_ = (hh * NT + kk) % 6
                MM(bank(pvb, 65), PT[sl_][:, (kb - kk) * 128:(kb - kk + 1) * 128], VP0[buf][:, kk, hh, :],
                   kk == k0, kk == kb, [("PT", sl_), ("VP", buf, kk // 4), ("VPones", buf)], [pk(pvb)])
            c = state["ss"] % 8
            state["ss"] += 1
            rc = ST[:, c:c + 1]
            src = bank(pvb, 1, 64)
            P.op("dve", lambda e: e.reciprocal(out=rc, in_=src), [pk(pvb)], [("ss", c)])
            TS(OBT[:, kb, h * 64:(h + 1) * 64], bank(pvb, 64), rc, ALU.mult, [pk(pvb), ("ss", c)] + fr(), [akey(("OBT", kb))])

        def kcopy(g_):
            hh, kb = divmod(g_, NT)
            pr = slice(hh * 64, hh * 64 + 64)
            CP(KPAD[hh][kb % 4][pr, :], KT[buf][pr, kb * 128:(kb + 1) * 128],
               [("KT", buf, kb // 4), ("KPADZ", hh)] + fr(), [akey(("KPAD", hh, kb % 4))])

        kcopy(0)
        kcopy(1)
        s1(0)
        for g_ in range(NG):
            if g_ + 2 < NG:
                kcopy(g_ + 2)
            if g_ + 1 < NG:
                s1(g_ + 1)
            yield
            s2(g_)

    def attn1_pair(hp, buf):
        ZV = PS[:, 1024:2048].rearrange("p (h c) -> p h c", h=2)
        P2V = PS[:, 2048:3072].rearrange("p (h c) -> p h c", h=2)

        def cz(qb0, kb):
            return max(0, kb - qb0) * 128

        def z_mm(qb0, kb):
            c0 = cz(qb0, kb)
            for hh in range(2):
                pr = slice(hh * 64, hh * 64 + 64)
                MM(bank(2 + hh, 512 - c0, c0), KT[buf][pr, kb * 128:(kb + 1) * 128],
                   QT[buf][pr, qb0 * 128 + c0:qb0 * 128 + 512], True, True,
                   [("KT", buf, kb // 4), ("QT", buf, qb0 // 4)], [pk(2 + hh)])

        def exp_e(qb0, kb, par):
            c0 = cz(qb0, kb)
            ek = akey(("E1", par))
            ACT(E1[par][:, :, c0:512], ZV[:, :, c0:512], AF.Exp, [pk(2), pk(3)] + fr(), [ek])
            if kb >= qb0:
                TT(E1[par][:, :, c0:c0 + 128], E1[par][:, :, c0:c0 + 128], MASK2, ALU.mult, [ek, "cf"], [ek])

        def ln_l(qb0, kb, pe_, par):
            c0 = cz(qb0, kb)
            ACT(L1[par][:, :, c0:512], E1[pe_][:, :, c0:512], AF.Ln, [("E1", pe_)] + fr(), [akey(("L1", par))], bias=1.0)

        def tri(qb0, kb, par, first, last):
            c0 = cz(qb0, kb)
            for hh in range(2):
                MM(bank(4 + hh, 512 - c0, c0), TRI[:], L1[par][:, hh, c0:512], first, last,
                   [("L1", par), "tri"], [pk(4 + hh)], skip=True)

        def ex2(qb0, kb):
            c0 = cz(qb0, kb)
            ACT(EX2[:, :, c0:512], P2V[:, :, c0:512], AF.Exp, [pk(4), pk(5)] + fr(), [akey("EX2")], scale=-1.0)

        def tric(qb0, kb, par, prelast):
            c0 = cz(qb0, kb)
            for hh in range(2):
                MM(bank(4 + hh, 512 - c0, c0), TRIC[:], L1[par][:, hh, c0:512], False, prelast,
                   [("L1", par), "tric"], [pk(4 + hh)], skip=True)

        def a_prod(qb0, kb, pe_, par):
            c0 = cz(qb0, kb)
            TT(A1[par][:, :, c0:512], E1[pe_][:, :, c0:512], EX2[:, :, c0:512], ALU.mult,
               [("E1", pe_), "EX2"] + fr(), [akey(("A1", par))])

        def av(qb0, kb, par, first, last):
            c0 = cz(qb0, kb)
            for hh in range(2):
                MM(bank(6 + hh, 512 - c0, c0), VP1[buf][:, kb, :], A1[par][:, hh, c0:512], first, last,
                   [("A1", par), ("VP", buf, kb // 4)], [pk(6 + hh)], skip=True)

        ticks = []
        for qb0 in range(0, NT, 4):
            ns = qb0 + 4
            for k in range(ns):
                ticks.append((qb0, qb0 + 3 - k, k, ns))
        ng = len(ticks)
        z_mm(ticks[0][0], ticks[0][1])
        exp_e(ticks[0][0], ticks[0][1], 0)
        z_mm(ticks[1][0], ticks[1][1])
        for g_ in range(ng):
            qb0, kb, k, ns = ticks[g_]
            ln_l(qb0, kb, g_ % 3, g_ % 2)
            tri(qb0, kb, g_ % 2, k == 0, k == ns - 1)
            if g_ + 1 < ng:
                exp_e(ticks[g_ + 1][0], ticks[g_ + 1][1], (g_ + 1) % 3)
            ex2(qb0, kb)
            if k >= 1:
                av(qb0, ticks[g_ - 1][1], (g_ - 1) % 2, k - 1 == 0, False)
            if g_ + 2 < ng:
                z_mm(ticks[g_ + 2][0], ticks[g_ + 2][1])
            yield
            if k < ns - 1:
                tric(qb0, kb, g_ % 2, k == ns - 2)
            a_prod(qb0, kb, g_ % 3, g_ % 2)
            if k == ns - 1:
                av(qb0, kb, g_ % 2, False, True)
                for hh in range(2):
                    pr = slice(hh * 64, hh * 64 + 64)
                    CP(OBF[pr, hp, qb0 * 128:qb0 * 128 + 512], PS[pr, (6 + hh) * 512:(7 + hh) * 512],
                       [pk(6 + hh)] + fr(), [akey(("OBF", hp, qb0 // 4))])

    def mixer(l):
        layer0 = (l % 2 == 0)
        g = load_gain(l * 4 + 0)
        norm_tiles(range(NT), g, HT, lambda i: i * 128, lambda i: ("HT", i // 4))
        if layer0:
            for b_ in range(2):
                MS(VP0[b_][:, :, :, 64:65], 1.0, fr(), [akey(("VPones", b_))])
            for h_ in range(2):
                for r in range(4):
                    MS(KPAD[h_][r][(1 - h_) * 64:(2 - h_) * 64, :], 0.0, fr(), [akey(("KPADZ", h_))])
        slabs = {}

        def get_slabs(hp):
            gq = hp // 2
            if gq not in slabs:
                slabs[gq] = tuple(load_slab(wqkv_d[l][:, part * D + gq * 256: part * D + gq * 256 + 256]) for part in range(3))
            return slabs[gq]

        get_slabs(0)
        get_slabs(2)
        for _ in proj_pair(l, 0, get_slabs(0), 0, layer0):
            pass
        for hp in range(8):
            nxt = proj_pair(l, hp + 1, get_slabs(hp + 1), (hp + 1) % 2, layer0) if hp + 1 < 8 else None
            if layer0:
                interleave(attn0_pair(hp, hp % 2), nxt, 4)
            else:
                interleave(attn1_pair(hp, hp % 2), nxt, 4)
            if hp % 2 == 1 and hp + 3 < 8:
                get_slabs(hp + 3)
        wos = [load_slab(wo_d[l][:, c4 * 256:(c4 + 1) * 256]) for c4 in range(4)]
        g = load_gain(l * 4 + 1)

        def wo_tile(i, OT, okeys):
            b0 = 2 + 2 * (i % 3)
            for c4 in range(4):
                for kc in range(KC):
                    MM(bank(b0 + c4 // 2, 256, (c4 % 2) * 256), OT[:, kc, i * 128:(i + 1) * 128], SLAB[wos[c4]][:, kc, :],
                       kc == 0, kc == KC - 1, okeys + [("slab", wos[c4])], [pk(b0 + c4 // 2)])
            post_norm_residual(i, b0, g)

        if layer0:
            def tr_tile(i):
                pb = state["nb"] % 2
                state["nb"] += 1
                pbf = bank_bf(pb)
                for kc in range(KC):
                    TR(pbf[:, kc * 128:(kc + 1) * 128], OBT[:, i, kc * 128:(kc + 1) * 128], [("OBT", i)], [pk(pb)])
                ACT(HT[:, :, i * 128:(i + 1) * 128], pbf.rearrange("p (a b) -> p a b", b=128), AF.Copy,
                    [pk(pb)] + fr(), [akey(("HT", i // 4)), akey(("OT0", i))])

            tr_tile(0)
            for i in range(NT):
                if i + 1 < NT:
                    tr_tile(i + 1)
                wo_tile(i, HT, [("OT0", i)])
        else:
            for i in range(NT):
                wo_tile(i, OBF, [("OBF", hp_, i // 4) for hp_ in range(8)])

    def ffn(l, tile_hook=None):
        cur_fence[0] = fence()
        g = load_gain(l * 4 + 2)
        g2 = load_gain(l * 4 + 3)
        norm_tiles(range(8), g, H2, lambda i: i * 128, lambda i: ("H2", i // 4))
        for half in range(2):
            for sp_ in range(11):
                sg = load_slab(wg_d[l][:, sp_ * 256:(sp_ + 1) * 256])
                su = load_slab(wu_d[l][:, sp_ * 256:(sp_ + 1) * 256])
                if half == 0 and sp_ == 0:
                    for j in range(6):
                        k0 = j * 4
                        k1 = min(FC, k0 + 4)
                        DMA("pool", WD[:, k0:k1, :], wd_d[l][k0 * 128:k1 * 128, :].rearrange("(kc p) n -> p kc n", p=128),
                            fr(), [akey(("WD", j))], ("WD", j))
                for f2 in range(2):
                    fb = sp_ * 2 + f2
                    for tb in range(2):
                        gb = 0 + 2 * tb
                        ub = 1 + 2 * tb
                        for (sl, bb) in ((sg, gb), (su, ub)):
                            for kc in range(KC):
                                MM(bank(bb), SLAB[sl][:, kc, f2 * 128:(f2 + 1) * 128], H2[:, kc, tb * 512:(tb + 1) * 512],
                                   kc == 0, kc == KC - 1, [("slab", sl), ("H2", tb)], [pk(bb)])
                        t = state["tmp"] % 2
                        state["tmp"] += 1
                        ACT(TMP[t][:], bank(gb), AF.Silu, [pk(gb)], [("tmp", t)])
                        TT(ACTT[:, fb, tb * 512:(tb + 1) * 512], bank(ub), TMP[t][:], ALU.mult,
                           [pk(ub), ("tmp", t)] + fr(), [akey(("ACTT", fb, tb))])
            for il in range(8):
                i = half * 8 + il
                b0 = 4 + 2 * (il % 2)
                for kc in range(FC):
                    for hf in range(2):
                        MM(bank(b0 + hf), ACTT[:, kc, il * 128:(il + 1) * 128], WD[:, kc, hf * 512:(hf + 1) * 512],
                           kc == 0, kc == FC - 1, [("ACTT", kc, il // 4), ("WD", kc // 4)], [pk(b0 + hf)])
                if half == 0:
                    norm_tile(8 + il, g, H2, il * 128, ("H2", il // 4))
                post_norm_residual(i, b0, g2)
                if tile_hook is not None:
                    tile_hook(i)
        cur_fence[0] = fence()

    final = []
    xrow = lambda ap, i0, n: ap[i0 * 128:(i0 + n) * 128, :].rearrange("(i p) d -> p i d", p=128)
    for i in range(NT):
        DMA("sp", X[:, i:i + 1, :], xrow(x_d[0], i, 1), [], [("X", i)], ("x0", i))
    for s in range(nseq):
        last_l = layers[-1]

        def hook(i, s=s):
            if i % 4 != 3:
                return
            j = i // 4
            keys = [("X", j * 4 + t) for t in range(4)]
            o = DMA("sp", xrow(y_d[s], j * 4, 4), X[:, j * 4:(j + 1) * 4, :], keys, [], ("y", j))
            final.append(o)
            if s + 1 < nseq:
                DMA("sp", X[:, j * 4:(j + 1) * 4, :], xrow(x_d[s + 1], j * 4, 4), [], keys, ("x", j))

        for l in layers:
            if do_mix:
                mixer(l)
            if do_ffn:
                ffn(l, hook if l == last_l else None)
        if not do_ffn:
            for i in range(NT):
                hook(i)
    stats = P.emit(final_waits=final[-4:] if len(final) >= 4 else final)
    return nc, stats


def host_consts(rel_bias):
    j = np.arange(128)[:, None]
    s = np.arange(128)[None, :]
    consts = np.zeros((5, 128, 128), np.float32)
    consts[0] = np.eye(128, dtype=np.float32)
    consts[1] = (j >= s).astype(np.float32)
    consts[2] = (j < s).astype(np.float32)
    consts[3] = (j < s).astype(np.float32)
    consts[4] = consts[3]
    sl = np.arange(128)[:, None]
    tl = np.arange(256)[None, :]
    idx = np.clip(tl - sl, -63, 128) + 63
    rb = np.asarray(rel_bias, np.float32)[0]
    tiles = rb[:, idx]
    invisible = (sl >= 64) & (tl < 64)
    tiles = np.where(invisible[None], np.float32(NEG), tiles).astype(np.float32)
    cb = np.broadcast_to(rb[:, 191][None, :], (128, NH)).astype(np.float32).copy()
    return consts, np.ascontiguousarray(tiles), cb


def host_gains(g_pre_mix, g_post_mix, g_pre_ffn, g_post_ffn):
    out = np.zeros((8, 128, D), np.float32)
    for l in range(2):
        for k, g in enumerate((g_pre_mix, g_post_mix, g_pre_ffn, g_post_ffn)):
            out[l * 4 + k] = np.broadcast_to(np.asarray(g, np.float32)[l][None, :], (128, D))
    return out


_CACHE = {}


def kernel(x, g_pre_mix, g_post_mix, w_qkv, w_o, rel_bias, g_pre_ffn, g_post_ffn, w_gate, w_up, w_down):
    n = 8
    x = np.asarray(x, np.float32)
    if "nc" not in _CACHE:
        _CACHE["nc"] = build_program(nseq=2)[0]
    nc = _CACHE["nc"]
    consts, tiles, cb = host_consts(rel_bias)
    gains = host_gains(g_pre_mix, g_post_mix, g_pre_ffn, g_post_ffn)
    shared = {
        "w_qkv": np.ascontiguousarray(np.asarray(w_qkv, np.float32)),
        "w_o": np.ascontiguousarray(np.asarray(w_o, np.float32)),
        "w_gate": np.ascontiguousarray(np.asarray(w_gate, np.float32)),
        "w_up": np.ascontiguousarray(np.asarray(w_up, np.float32)),
        "w_down": np.ascontiguousarray(np.asarray(w_down, np.float32)),
        "gains": gains, "bias_t": tiles, "cbias": cb, "consts": consts,
    }
    in_maps = []
    for c in range(n):
        m = dict(shared)
        m["x"] = np.ascontiguousarray(x[2 * c:2 * c + 2])
        in_maps.append(m)
    res = run_bass_kernel_spmd(nc, in_maps, core_ids=list(range(n)))
    out = np.concatenate([np.asarray(r["y"], np.float32) for r in res.results], axis=0)
    return out
```

```python
import numpy as np
import concourse.bass as bass
import concourse.mybir as mybir
from concourse.bass_utils import run_bass_kernel_spmd

F32 = mybir.dt.float32
BF16 = mybir.dt.bfloat16
AF = mybir.ActivationFunctionType
ALU = mybir.AluOpType

D = 1024
S = 2048
NH = 16
DFF = 2816
NT = 16
KC = 8
FC = 22
EPS = 1e-6
NEG = -30000.0

ENGS = ("pe", "act", "dve", "pool", "sp")


class Op:
    __slots__ = ("eng", "fn", "deps", "is_dma", "dma_sem", "dma_val", "sig", "idx")

    def __init__(self, eng, fn, deps, is_dma):
        self.eng = eng
        self.fn = fn
        self.deps = deps
        self.is_dma = is_dma
        self.dma_sem = None
        self.dma_val = 0
        self.sig = 0
        self.idx = -1


class Prog:
    def __init__(self, nc):
        self.nc = nc
        self.ops = []
        self.per_eng = {e: [] for e in ENGS}
        self.last_w = {}
        self.readers = {}
        self.dma_count = {}
        self.same_eng_window = 6
        self._n = 0

    def op(self, eng, fn, reads=(), writes=(), dma_slot=None):
        psr = [k for k in reads if isinstance(k, tuple) and k[0] == "ps"]
        if psr:
            reads = [k for k in reads if not (isinstance(k, tuple) and k[0] == "ps")]
            writes = list(writes) + [k for k in psr if k not in writes]
        deps = set()
        for k in reads:
            w = self.last_w.get(k)
            if w is not None:
                deps.add(w)
        for k in writes:
            w = self.last_w.get(k)
            if w is not None:
                deps.add(w)
            rd = self.readers.get(k)
            if rd:
                deps.update(rd.values())
        is_dma = dma_slot is not None
        o = Op(eng, fn, deps, is_dma)
        if is_dma:
            c = self.dma_count.get(dma_slot, 0) + 16
            self.dma_count[dma_slot] = c
            o.dma_sem = dma_slot
            o.dma_val = c
        o.idx = len(self.per_eng[eng])
        self.per_eng[eng].append(o)
        self.ops.append(o)
        for k in writes:
            self.last_w[k] = o
            self.readers[k] = {}
        self._n += 1
        for k in reads:
            rd = self.readers.setdefault(k, {})
            rd[(eng, self._n) if is_dma else eng] = o
        return o

    def emit(self, final_waits=()):
        nc = self.nc
        needed = set()
        for o in self.ops:
            eff = []
            for d in o.deps:
                if not d.is_dma and d.eng == o.eng and not o.is_dma:
                    if o.eng == "pe":
                        continue
                    if o.idx - d.idx > self.same_eng_window:
                        continue
                eff.append(d)
                if not d.is_dma:
                    needed.add(d)
            o.deps = eff
        for o in final_waits:
            if not o.is_dma:
                needed.add(o)
        for e in ENGS:
            c = 0
            for o in self.per_eng[e]:
                if o in needed:
                    c += 1
                    o.sig = c
        sems = {e: nc.alloc_semaphore("prog_" + e) for e in ENGS}
        dsems = {}
        for i, k in enumerate(self.dma_count):
            dsems[k] = nc.alloc_semaphore("dma%d" % i)
        engobj = {"pe": "tensor", "act": "scalar", "dve": "vector", "pool": "gpsimd", "sp": "sync"}
        per_eng = self.per_eng
        stats = {e: [len(per_eng[e]), 0] for e in ENGS}

        def make(e):
            def body(eng):
                waited = {}
                for o in per_eng[e]:
                    for d in o.deps:
                        if d.is_dma:
                            key = ("d", d.dma_sem)
                            if waited.get(key, 0) < d.dma_val:
                                eng.wait_ge(dsems[d.dma_sem], d.dma_val)
                                waited[key] = d.dma_val
                                stats[e][1] += 1
                        else:
                            key = ("e", d.eng)
                            if waited.get(key, 0) < d.sig:
                                eng.wait_ge(sems[d.eng], d.sig)
                                waited[key] = d.sig
                                stats[e][1] += 1
                    ins = o.fn(eng)
                    if o.is_dma:
                        ins.then_inc(dsems[o.dma_sem], 16)
                    elif o.sig:
                        ins.then_inc(sems[e], 1)
                if e == "sp":
                    for o in final_waits:
                        if o.is_dma:
                            eng.wait_ge(dsems[o.dma_sem], o.dma_val)
                        else:
                            eng.wait_ge(sems[o.eng], o.sig)
            return body

        with nc.Block() as block:
            for e in ENGS:
                if per_eng[e] or e == "sp":
                    getattr(block, engobj[e])(make(e))
        self.stats = stats
        return stats


def interleave(main, side, n_side):
    side_done = side is None
    for _ in main:
        if not side_done:
            for _i in range(n_side):
                try:
                    next(side)
                except StopIteration:
                    side_done = True
                    break
    if not side_done:
        for _ in side:
            pass


def build_program(nseq=2, layers=(0, 1), do_ffn=True, do_mix=True):
    nc = bass.Bass("TRN2", target_bir_lowering=False, dynamic_dma_scratch_size=4096)
    dt = lambda name, shape, kind="ExternalInput": nc.dram_tensor(name, shape, F32, kind=kind).ap()
    x_d = dt("x", [nseq, S, D])
    y_d = dt("y", [nseq, S, D], kind="ExternalOutput")
    wqkv_d = dt("w_qkv", [2, D, 3 * D])
    wo_d = dt("w_o", [2, D, D])
    wg_d = dt("w_gate", [2, D, DFF])
    wu_d = dt("w_up", [2, D, DFF])
    wd_d = dt("w_down", [2, DFF, D])
    gains_d = dt("gains", [8, 128, D])
    bias_d = dt("bias_t", [NH, 128, 256])
    cb_d = dt("cbias", [128, NH])
    consts_d = dt("consts", [5, 128, 128])

    P = Prog(nc)

    X = nc.alloc_sbuf_tensor("X", [128, NT, D], F32)
    ARENA_B = 106624
    ARENA = nc.alloc_sbuf_tensor("ARENA", [128, ARENA_B // 2], BF16)
    SLAB = [nc.alloc_sbuf_tensor("slab%d" % i, [128, KC, 256], BF16) for i in range(6)]
    G = [nc.alloc_sbuf_tensor("G%d" % i, [128, D], F32) for i in range(2)]
    TMP = [nc.alloc_sbuf_tensor("tmp%d" % i, [128, 512], F32) for i in range(2)]
    JUNK = nc.alloc_sbuf_tensor("junk", [128, D], BF16)
    XS = [nc.alloc_sbuf_tensor("xs%d" % i, [128, D], BF16) for i in range(2)]
    CF = nc.alloc_sbuf_tensor("cf", [128, 5, 128], F32)
    IDB = nc.alloc_sbuf_tensor("idb", [128, 128], BF16)
    TRI = nc.alloc_sbuf_tensor("tri", [128, 128], BF16)
    TRIC = nc.alloc_sbuf_tensor("tric", [128, 128], BF16)
    CB = nc.alloc_sbuf_tensor("cb", [128, NH], F32)
    ST = nc.alloc_sbuf_tensor("st", [128, 16], F32)
    MASK2 = CF[:, 3:5, :]

    def aview(off, shape, dtype=BF16):
        n = 1
        for s_ in shape[1:]:
            n *= s_
        esz = 2 if dtype == BF16 else 4
        ap = ARENA[:, off // 2: off // 2 + n * esz // 2]
        if dtype != BF16:
            ap = ap.bitcast(dtype)
        if len(shape) == 3:
            ap = ap.rearrange("p (a b) -> p a b", b=shape[2])
        elif len(shape) == 4:
            ap = ap.rearrange("p (a b c) -> p a b c", b=shape[2], c=shape[3])
        return ap

    HT = aview(0, [128, KC, S])
    OBF = aview(32768, [128, KC, S])
    OBT = aview(32768, [128, NT, D])
    QT = [aview(65536 + i * 8192, [128, S]) for i in range(2)]
    KT = [aview(65536 + i * 8192 + 4096, [128, S]) for i in range(2)]
    VP0 = [aview(81920 + i * 4160, [128, NT, 2, 65]) for i in range(2)]
    VP1 = [aview(81920 + i * 4160, [128, NT, 128]) for i in range(2)]
    TB = 90240
    PT = [aview(TB + i * 1280, [128, 640]) for i in range(6)]
    ZB = [aview(TB + 7680 + i * 1024, [128, 256], F32) for i in range(2)]
    BT = [aview(TB + 9728 + i * 1024, [128, 256], F32) for i in range(2)]
    KPAD = [[aview(TB + 11776 + (h_ * 4 + r_) * 256, [128, 128]) for r_ in range(4)] for h_ in range(2)]
    E1 = [aview(TB + p * 2048, [128, 2, 512]) for p in range(3)]
    L1 = [aview(TB + 6144 + p * 2048, [128, 2, 512]) for p in range(2)]
    EX2 = aview(TB + 10240, [128, 2, 512])
    A1 = [aview(TB + 12288 + p * 2048, [128, 2, 512]) for p in range(2)]
    H2 = aview(0, [128, KC, 1024])
    ACTT = aview(16384, [128, FC, 1024])
    WD = aview(61440, [128, FC, D])

    PS = nc.alloc_psum_tensor("ps", [128, 4096], F32)

    def bank(b, n=512, c0=0):
        return PS[:, b * 512 + c0: b * 512 + c0 + n]

    def bank_bf(b):
        return PS[:, b * 512:(b + 1) * 512].bitcast(BF16)

    pk = lambda b: ("ps", b)

    P.op("sp", lambda e: e.dma_start(out=CF[:], in_=consts_d.rearrange("c p n -> p c n")), writes=["cf"], dma_slot="c0")
    P.op("sp", lambda e: e.dma_start(out=CB[:], in_=cb_d), writes=["cb"], dma_slot="c1")
    P.op("dve", lambda e: e.tensor_copy(out=IDB[:], in_=CF[:, 0, :]), reads=["cf"], writes=["idb"])
    P.op("dve", lambda e: e.tensor_copy(out=TRI[:], in_=CF[:, 1, :]), reads=["cf"], writes=["tri"])
    P.op("dve", lambda e: e.tensor_copy(out=TRIC[:], in_=CF[:, 2, :]), reads=["cf"], writes=["tric"])

    state = {"slab": 0, "g": 0, "ss": 0, "xs": 0, "tmp": 0, "nb": 0, "fence": 0}
    arena_keys = set()

    def akey(k):
        arena_keys.add(k)
        return k

    def fence():
        state["fence"] += 1
        fk = ("fence", state["fence"])
        keys = list(arena_keys)
        P.op("dve", lambda e: e.memset(ST[:, 15:16], 0.0), reads=[], writes=keys + [fk])
        arena_keys.clear()
        return fk

    cur_fence = [None]

    def fr():
        return [cur_fence[0]] if cur_fence[0] is not None else []

    def MM(out, lhsT, rhs, start, stop, reads, writes, skip=False):
        return P.op("pe", lambda e: e.matmul(out, lhsT=lhsT, rhs=rhs, start=start, stop=stop, skip_group_check=skip),
                    reads, writes)

    def TR(out, in_, reads, writes):
        return P.op("pe", lambda e: e.transpose(out=out, in_=in_, identity=IDB[:]), list(reads) + ["idb"], writes)

    def ACT(out, in_, func, reads, writes, **kw):
        return P.op("act", lambda e: e.activation(out=out, in_=in_, func=func, **kw), reads, writes)

    def TT(out, in0, in1, op, reads, writes):
        return P.op("dve", lambda e: e.tensor_tensor(out=out, in0=in0, in1=in1, op=op), reads, writes)

    def TS(out, in0, s1, op0, reads, writes):
        return P.op("dve", lambda e: e.tensor_scalar(out=out, in0=in0, scalar1=s1, scalar2=None, op0=op0), reads, writes)

    def STT(out, in0, scalar, in1, op0, op1, reads, writes):
        return P.op("dve", lambda e: e.scalar_tensor_tensor(out=out, in0=in0, scalar=scalar, in1=in1, op0=op0, op1=op1),
                    reads, writes)

    def CP(out, in_, reads, writes):
        return P.op("dve", lambda e: e.tensor_copy(out=out, in_=in_), reads, writes)

    def MS(ap, val, reads, writes):
        return P.op("dve", lambda e: e.memset(ap, val), reads, writes)

    def DMA(eng, out, in_, reads, writes, slot):
        return P.op(eng, lambda e: e.dma_start(out=out, in_=in_), reads, writes, dma_slot=slot)

    def load_slab(src_ap):
        r = state["slab"] % 6
        state["slab"] += 1
        DMA("pool", SLAB[r][:], src_ap.rearrange("(kc p) n -> p kc n", p=128), [], [("slab", r)], ("slab", r))
        return r

    def load_gain(idx):
        g = state["g"] % 2
        state["g"] += 1
        DMA("sp", G[g][:], gains_d[idx], [], [("G", g)], ("G", g))
        return g

    def rstd_ops(src_ap, src_keys):
        c = state["ss"] % 8
        state["ss"] += 1
        ss = ST[:, c:c + 1]
        k = ("ss", c)
        ACT(JUNK[:], src_ap, AF.Square, src_keys, [k], accum_out=ss)
        ACT(ss, ss, AF.Ln, [k], [k], scale=1.0 / D, bias=EPS)
        ACT(ss, ss, AF.Exp, [k], [k], scale=-0.5)
        return ss, k

    def norm_a(i, g):
        rs, rk = rstd_ops(X[:, i, :], [("X", i)])
        b = state["xs"] % 2
        state["xs"] += 1
        xs = XS[b]
        STT(xs[:], X[:, i, :], rs, G[g][:], ALU.mult, ALU.mult, [("X", i), rk, ("G", g)], [("xs", b)])
        pb = state["nb"] % 2
        state["nb"] += 1
        pbf = bank_bf(pb)
        for kc in range(KC):
            TR(pbf[:, kc * 128:(kc + 1) * 128], xs[:, kc * 128:(kc + 1) * 128], [("xs", b)], [pk(pb)])
        return pb

    def norm_b(pb, dst, dcol, dkey):
        ACT(dst[:, :, dcol:dcol + 128], bank_bf(pb).rearrange("p (a b) -> p a b", b=128), AF.Copy, [pk(pb)] + fr(), [akey(dkey)])

    def norm_tile(i, g, dst, dcol, dkey):
        norm_b(norm_a(i, g), dst, dcol, dkey)

    def norm_tiles(tiles, g, dst, dcol_of, dkey_of):
        prev = None
        for i in tiles:
            pb = norm_a(i, g)
            if prev is not None:
                norm_b(prev[0], dst, dcol_of(prev[1]), dkey_of(prev[1]))
            prev = (pb, i)
        norm_b(prev[0], dst, dcol_of(prev[1]), dkey_of(prev[1]))

    def post_norm_residual(i, b0, g):
        src = PS[:, b0 * 512: b0 * 512 + 1024]
        rs, rk = rstd_ops(src, [pk(b0), pk(b0 + 1)])
        for hf in range(2):
            t = state["tmp"] % 2
            state["tmp"] += 1
            xa = X[:, i, hf * 512:(hf + 1) * 512]
            STT(TMP[t][:], bank(b0 + hf), rs, G[g][:, hf * 512:(hf + 1) * 512], ALU.mult, ALU.mult,
                [pk(b0 + hf), rk, ("G", g)], [("tmp", t)])
            TT(xa, xa, TMP[t][:], ALU.add, [("tmp", t), ("X", i)], [("X", i)])

    def proj_pair(l, hp, slabs, buf, layer0):
        sq, sk, sv = slabs
        lo = (hp % 2) * 128
        for (sl, dstT, scale, nm) in ((sq, QT[buf], 0.125, "QT"), (sk, KT[buf], None, "KT")):
            for tb in range(4):
                pb = state["nb"] % 2
                state["nb"] += 1
                for kc in range(KC):
                    MM(bank(pb), SLAB[sl][:, kc, lo:lo + 128], HT[:, kc, tb * 512:(tb + 1) * 512], kc == 0, kc == KC - 1,
                       [("slab", sl), ("HT", tb)], [pk(pb)])
                    if kc < KC - 1:
                        yield
                if scale is not None:
                    TS(dstT[:, tb * 512:(tb + 1) * 512], bank(pb), scale, ALU.mult, [pk(pb)] + fr(), [akey((nm, buf, tb))])
                else:
                    CP(dstT[:, tb * 512:(tb + 1) * 512], bank(pb), [pk(pb)] + fr(), [akey((nm, buf, tb))])
                yield
        for t4 in range(4):
            pb = state["nb"] % 2
            state["nb"] += 1
            for j in range(4):
                i = t4 * 4 + j
                for kc in range(KC):
                    MM(bank(pb, 128, j * 128), HT[:, kc, i * 128:(i + 1) * 128], SLAB[sv][:, kc, lo:lo + 128],
                       kc == 0, kc == KC - 1, [("slab", sv), ("HT", i // 4)], [pk(pb)])
                    if kc % 2 == 1 and not (j == 3 and kc == KC - 1):
                        yield
            if layer0:
                CP(VP0[buf][:, t4 * 4:(t4 + 1) * 4, :, 0:64], bank(pb).rearrange("p (a b c) -> p a b c", b=2, c=64),
                   [pk(pb)] + fr(), [akey(("VP", buf, t4))])
            else:
                CP(VP1[buf][:, t4 * 4:(t4 + 1) * 4, :], bank(pb).rearrange("p (a b) -> p a b", b=128),
                   [pk(pb)] + fr(), [akey(("VP", buf, t4))])
            yield

    def attn0_pair(hp, buf):
        for hh in range(2):
            h = hp * 2 + hh
            DMA("sp", BT[h % 2], bias_d[h], fr(), [akey(("BT", h % 2))], ("BT", h % 2))
        NG = 2 * NT

        def s1(g_):
            hh, kb = divmod(g_, NT)
            h = hp * 2 + hh
            bt = h % 2
            q0 = kb * 128
            nq = min(640, S - q0)
            zb = 2 + 2 * (g_ % 2)
            pt = PT[g_ % 6]
            ptk = akey(("PT", g_ % 6))
            kp = KPAD[hh][kb % 4]
            kpk = ("KPAD", hh, kb % 4)
            nb_ = min(256, nq)
            MM(bank(zb, nb_), kp[:], QT[buf][:, q0:q0 + nb_], True, True,
               [kpk] + [("QT", buf, t) for t in range(q0 // 512, (q0 + nb_ - 1) // 512 + 1)], [pk(zb)])
            if nq > 256:
                MM(bank(zb + 1, nq - 256), kp[:], QT[buf][:, q0 + 256:q0 + nq], True, True,
                   [kpk] + [("QT", buf, t) for t in range((q0 + 256) // 512, (q0 + nq - 1) // 512 + 1)],
                   [pk(zb + 1)])
            z2 = g_ % 2
            if nq > 256:
                ACT(pt[:, 256:nq], bank(zb + 1, nq - 256), AF.Exp, [pk(zb + 1), "cb"] + fr(), [ptk], bias=CB[:, h:h + 1])
            if nq > 576:
                MS(pt[0:64, 576:640], 0.0, [], [ptk])
            TT(ZB[z2][:, 0:nb_], bank(zb, nb_), BT[bt][:, 0:nb_], ALU.add, [pk(zb), ("BT", bt)] + fr(), [akey(("ZB", z2))])
            ACT(pt[:, 0:nb_], ZB[z2][:, 0:nb_], AF.Exp, [("ZB", z2)], [ptk])

        def s2(g_):
            hh, kb = divmod(g_, NT)
            h = hp * 2 + hh
            pvb = 6 + (g_ % 2)
            k0 = max(0, kb - 4)
            for kk in range(k0, kb + 1):
                sl_ = (hh * NT + kk) % 6
                MM(bank(pvb, 65), PT[sl_][:, (kb - kk) * 128:(kb - kk + 1) * 128], VP0[buf][:, kk, hh, :],
                   kk == k0, kk == kb, [("PT", sl_), ("VP", buf, kk // 4), ("VPones", buf)], [pk(pvb)])
            c = state["ss"] % 8
            state["ss"] += 1
            rc = ST[:, c:c + 1]
            src = bank(pvb, 1, 64)
            P.op("dve", lambda e: e.reciprocal(out=rc, in_=src), [pk(pvb)], [("ss", c)])
            TS(OBT[:, kb, h * 64:(h + 1) * 64], bank(pvb, 64), rc, ALU.mult, [pk(pvb), ("ss", c)] + fr(), [akey(("OBT", kb))])

        def kcopy(g_):
            hh, kb = divmod(g_, NT)
            pr = slice(hh * 64, hh * 64 + 64)
            CP(KPAD[hh][kb % 4][pr, :], KT[buf][pr, kb * 128:(kb + 1) * 128],
               [("KT", buf, kb // 4), ("KPADZ", hh)] + fr(), [akey(("KPAD", hh, kb % 4))])

        kcopy(0)
        kcopy(1)
        s1(0)
        for g_ in range(NG):
            if g_ + 2 < NG:
                kcopy(g_ + 2)
            if g_ + 1 < NG:
                s1(g_ + 1)
            yield
            s2(g_)

    def attn1_pair(hp, buf):
        ZV = PS[:, 1024:2048].rearrange("p (h c) -> p h c", h=2)
        P2V = PS[:, 2048:3072].rearrange("p (h c) -> p h c", h=2)

        def cz(qb0, kb):
            return max(0, kb - qb0) * 128

        def z_mm(qb0, kb):
            c0 = cz(qb0, kb)
            for hh in range(2):
                pr = slice(hh * 64, hh * 64 + 64)
                MM(bank(2 + hh, 512 - c0, c0), KT[buf][pr, kb * 128:(kb + 1) * 128],
                   QT[buf][pr, qb0 * 128 + c0:qb0 * 128 + 512], True, True,
                   [("KT", buf, kb // 4), ("QT", buf, qb0 // 4)], [pk(2 + hh)])

        def exp_e(qb0, kb, par):
            c0 = cz(qb0, kb)
            ek = akey(("E1", par))
            ACT(E1[par][:, :, c0:512], ZV[:, :, c0:512], AF.Exp, [pk(2), pk(3)] + fr(), [ek])
            if kb >= qb0:
                TT(E1[par][:, :, c0:c0 + 128], E1[par][:, :, c0:c0 + 128], MASK2, ALU.mult, [ek, "cf"], [ek])

        def ln_l(qb0, kb, pe_, par):
            c0 = cz(qb0, kb)
            ACT(L1[par][:, :, c0:512], E1[pe_][:, :, c0:512], AF.Ln, [("E1", pe_)] + fr(), [akey(("L1", par))], bias=1.0)

        def tri(qb0, kb, par, first, last):
            c0 = cz(qb0, kb)
            for hh in range(2):
                MM(bank(4 + hh, 512 - c0, c0), TRI[:], L1[par][:, hh, c0:512], first, last,
                   [("L1", par), "tri"], [pk(4 + hh)], skip=True)

        def ex2(qb0, kb):
            c0 = cz(qb0, kb)
            ACT(EX2[:, :, c0:512], P2V[:, :, c0:512], AF.Exp, [pk(4), pk(5)] + fr(), [akey("EX2")], scale=-1.0)

        def tric(qb0, kb, par, prelast):
            c0 = cz(qb0, kb)
            for hh in range(2):
                MM(bank(4 + hh, 512 - c0, c0), TRIC[:], L1[par][:, hh, c0:512], False, prelast,
                   [("L1", par), "tric"], [pk(4 + hh)], skip=True)

        def a_prod(qb0, kb, pe_, par):
            c0 = cz(qb0, kb)
            TT(A1[par][:, :, c0:512], E1[pe_][:, :, c0:512], EX2[:, :, c0:512], ALU.mult,
               [("E1", pe_), "EX2"] + fr(), [akey(("A1", par))])

        def av(qb0, kb, par, first, last):
            c0 = cz(qb0, kb)
            for hh in range(2):
                MM(bank(6 + hh, 512 - c0, c0), VP1[buf][:, kb, :], A1[par][:, hh, c0:512], first, last,
                   [("A1", par), ("VP", buf, kb // 4)], [pk(6 + hh)], skip=True)

        ticks = []
        for qb0 in range(0, NT, 4):
            ns = qb0 + 4
            for k in range(ns):
                ticks.append((qb0, qb0 + 3 - k, k, ns))
        ng = len(ticks)
        z_mm(ticks[0][0], ticks[0][1])
        exp_e(ticks[0][0], ticks[0][1], 0)
        z_mm(ticks[1][0], ticks[1][1])
        for g_ in range(ng):
            qb0, kb, k, ns = ticks[g_]
            ln_l(qb0, kb, g_ % 3, g_ % 2)
            tri(qb0, kb, g_ % 2, k == 0, k == ns - 1)
            if g_ + 1 < ng:
                exp_e(ticks[g_ + 1][0], ticks[g_ + 1][1], (g_ + 1) % 3)
            ex2(qb0, kb)
            if k >= 1:
                av(qb0, ticks[g_ - 1][1], (g_ - 1) % 2, k - 1 == 0, False)
            if g_ + 2 < ng:
                z_mm(ticks[g_ + 2][0], ticks[g_ + 2][1])
            yield
            if k < ns - 1:
                tric(qb0, kb, g_ % 2, k == ns - 2)
            a_prod(qb0, kb, g_ % 3, g_ % 2)
            if k == ns - 1:
                av(qb0, kb, g_ % 2, False, True)
                for hh in range(2):
                    pr = slice(hh * 64, hh * 64 + 64)
                    CP(OBF[pr, hp, qb0 * 128:qb0 * 128 + 512], PS[pr, (6 + hh) * 512:(7 + hh) * 512],
                       [pk(6 + hh)] + fr(), [akey(("OBF", hp, qb0 // 4))])

    pre_gain = {}

    def mixer(l):
        layer0 = (l % 2 == 0)
        g = pre_gain.pop("g", None)
        if g is None:
            g = load_gain(l * 4 + 0)
        norm_tiles(range(NT), g, HT, lambda i: i * 128, lambda i: ("HT", i // 4))
        if layer0:
            for b_ in range(2):
                MS(VP0[b_][:, :, :, 64:65], 1.0, fr(), [akey(("VPones", b_))])
            for h_ in range(2):
                for r in range(4):
                    MS(KPAD[h_][r][(1 - h_) * 64:(2 - h_) * 64, :], 0.0, fr(), [akey(("KPADZ", h_))])
        slabs = {}

        def get_slabs(hp):
            gq = hp // 2
            if gq not in slabs:
                slabs[gq] = tuple(load_slab(wqkv_d[l][:, part * D + gq * 256: part * D + gq * 256 + 256]) for part in range(3))
            return slabs[gq]

        get_slabs(0)
        get_slabs(2)
        for _ in proj_pair(l, 0, get_slabs(0), 0, layer0):
            pass
        for hp in range(8):
            nxt = proj_pair(l, hp + 1, get_slabs(hp + 1), (hp + 1) % 2, layer0) if hp + 1 < 8 else None
            if layer0:
                interleave(attn0_pair(hp, hp % 2), nxt, 4)
            else:
                interleave(attn1_pair(hp, hp % 2), nxt, 4)
            if hp % 2 == 1 and hp + 3 < 8:
                get_slabs(hp + 3)
        wos = [load_slab(wo_d[l][:, c4 * 256:(c4 + 1) * 256]) for c4 in range(4)]
        g = load_gain(l * 4 + 1)

        def wo_tile(i, OT, okeys):
            b0 = 2 + 2 * (i % 3)
            for c4 in range(4):
                for kc in range(KC):
                    MM(bank(b0 + c4 // 2, 256, (c4 % 2) * 256), OT[:, kc, i * 128:(i + 1) * 128], SLAB[wos[c4]][:, kc, :],
                       kc == 0, kc == KC - 1, okeys + [("slab", wos[c4])], [pk(b0 + c4 // 2)])
            post_norm_residual(i, b0, g)

        if layer0:
            def tr_tile(i):
                pb = state["nb"] % 2
                state["nb"] += 1
                pbf = bank_bf(pb)
                for kc in range(KC):
                    TR(pbf[:, kc * 128:(kc + 1) * 128], OBT[:, i, kc * 128:(kc + 1) * 128], [("OBT", i)], [pk(pb)])
                ACT(HT[:, :, i * 128:(i + 1) * 128], pbf.rearrange("p (a b) -> p a b", b=128), AF.Copy,
                    [pk(pb)] + fr(), [akey(("HT", i // 4)), akey(("OT0", i))])

            tr_tile(0)
            for i in range(NT):
                if i + 1 < NT:
                    tr_tile(i + 1)
                wo_tile(i, HT, [("OT0", i)])
        else:
            for i in range(NT):
                wo_tile(i, OBF, [("OBF", hp_, i // 4) for hp_ in range(8)])

    def ffn(l, tile_hook=None):
        cur_fence[0] = fence()
        g = load_gain(l * 4 + 2)
        g2 = load_gain(l * 4 + 3)
        norm_tiles(range(8), g, H2, lambda i: i * 128, lambda i: ("H2", i // 4))
        for half in range(2):
            for sp_ in range(11):
                sg = load_slab(wg_d[l][:, sp_ * 256:(sp_ + 1) * 256])
                su = load_slab(wu_d[l][:, sp_ * 256:(sp_ + 1) * 256])
                if half == 0 and sp_ == 0:
                    for j in range(6):
                        k0 = j * 4
                        k1 = min(FC, k0 + 4)
                        DMA("pool", WD[:, k0:k1, :], wd_d[l][k0 * 128:k1 * 128, :].rearrange("(kc p) n -> p kc n", p=128),
                            fr(), [akey(("WD", j))], ("WD", j))
                for f2 in range(2):
                    fb = sp_ * 2 + f2
                    for tb in range(2):
                        gb = 0 + 2 * tb
                        ub = 1 + 2 * tb
                        for (sl, bb) in ((sg, gb), (su, ub)):
                            for kc in range(KC):
                                MM(bank(bb), SLAB[sl][:, kc, f2 * 128:(f2 + 1) * 128], H2[:, kc, tb * 512:(tb + 1) * 512],
                                   kc == 0, kc == KC - 1, [("slab", sl), ("H2", tb)], [pk(bb)])
                        t = state["tmp"] % 2
                        state["tmp"] += 1
                        ACT(TMP[t][:], bank(gb), AF.Silu, [pk(gb)], [("tmp", t)])
                        TT(ACTT[:, fb, tb * 512:(tb + 1) * 512], bank(ub), TMP[t][:], ALU.mult,
                           [pk(ub), ("tmp", t)] + fr(), [akey(("ACTT", fb, tb))])
            for il in range(8):
                i = half * 8 + il
                b0 = 4 + 2 * (il % 2)
                for kc in range(FC):
                    for hf in range(2):
                        MM(bank(b0 + hf), ACTT[:, kc, il * 128:(il + 1) * 128], WD[:, kc, hf * 512:(hf + 1) * 512],
                           kc == 0, kc == FC - 1, [("ACTT", kc, il // 4), ("WD", kc // 4)], [pk(b0 + hf)])
                if half == 0:
                    norm_tile(8 + il, g, H2, il * 128, ("H2", il // 4))
                post_norm_residual(i, b0, g2)
                if tile_hook is not None:
                    tile_hook(i)
        cur_fence[0] = fence()

    final = []
    xrow = lambda ap, i0, n: ap[i0 * 128:(i0 + n) * 128, :].rearrange("(i p) d -> p i d", p=128)
    if do_mix:
        pre_gain["g"] = load_gain(layers[0] * 4 + 0)
    for i in range(NT):
        DMA("sp", X[:, i:i + 1, :], xrow(x_d[0], i, 1), [], [("X", i)], ("x0", i))
    for s in range(nseq):
        last_l = layers[-1]

        def hook(i, s=s):
            if i % 4 != 3:
                return
            j = i // 4
            if i == 11 and s + 1 < nseq and do_mix:
                pre_gain["g"] = load_gain(layers[0] * 4 + 0)
            keys = [("X", j * 4 + t) for t in range(4)]
            o = DMA("sp", xrow(y_d[s], j * 4, 4), X[:, j * 4:(j + 1) * 4, :], keys, [], ("y", j))
            final.append(o)
            if s + 1 < nseq:
                DMA("sp", X[:, j * 4:(j + 1) * 4, :], xrow(x_d[s + 1], j * 4, 4), [], keys, ("x", j))

        for l in layers:
            if do_mix:
                mixer(l)
            if do_ffn:
                ffn(l, hook if l == last_l else None)
        if not do_ffn:
            for i in range(NT):
                hook(i)
    stats = P.emit(final_waits=final[-4:] if len(final) >= 4 else final)
    return nc, stats


def host_consts(rel_bias):
    j = np.arange(128)[:, None]
    s = np.arange(128)[None, :]
    consts = np.zeros((5, 128, 128), np.float32)
    consts[0] = np.eye(128, dtype=np.float32)
    consts[1] = (j >= s).astype(np.float32)
    consts[2] = (j < s).astype(np.float32)
    consts[3] = (j < s).astype(np.float32)
    consts[4] = consts[3]
    sl = np.arange(128)[:, None]
    tl = np.arange(256)[None, :]
    idx = np.clip(tl - sl, -63, 128) + 63
    rb = np.asarray(rel_bias, np.float32)[0]
    tiles = rb[:, idx]
    invisible = (sl >= 64) & (tl < 64)
    tiles = np.where(invisible[None], np.float32(NEG), tiles).astype(np.float32)
    cb = np.broadcast_to(rb[:, 191][None, :], (128, NH)).astype(np.float32).copy()
    return consts, np.ascontiguousarray(tiles), cb


def host_gains(g_pre_mix, g_post_mix, g_pre_ffn, g_post_ffn):
    out = np.zeros((8, 128, D), np.float32)
    for l in range(2):
        for k, g in enumerate((g_pre_mix, g_post_mix, g_pre_ffn, g_post_ffn)):
            out[l * 4 + k] = np.broadcast_to(np.asarray(g, np.float32)[l][None, :], (128, D))
    return out


_CACHE = {}


def kernel(x, g_pre_mix, g_post_mix, w_qkv, w_o, rel_bias, g_pre_ffn, g_post_ffn, w_gate, w_up, w_down):
    n = 8
    x = np.asarray(x, np.float32)
    if "nc" not in _CACHE:
        _CACHE["nc"] = build_program(nseq=2)[0]
    nc = _CACHE["nc"]
    consts, tiles, cb = host_consts(rel_bias)
    gains = host_gains(g_pre_mix, g_post_mix, g_pre_ffn, g_post_ffn)
    shared = {
        "w_qkv": np.ascontiguousarray(np.asarray(w_qkv, np.float32)),
        "w_o": np.ascontiguousarray(np.asarray(w_o, np.float32)),
        "w_gate": np.ascontiguousarray(np.asarray(w_gate, np.float32)),
        "w_up": np.ascontiguousarray(np.asarray(w_up, np.float32)),
        "w_down": np.ascontiguousarray(np.asarray(w_down, np.float32)),
        "gains": gains, "bias_t": tiles, "cbias": cb, "consts": consts,
    }
    in_maps = []
    for c in range(n):
        m = dict(shared)
        m["x"] = np.ascontiguousarray(x[2 * c:2 * c + 2])
        in_maps.append(m)
    res = run_bass_kernel_spmd(nc, in_maps, core_ids=list(range(n)))
    out = np.concatenate([np.asarray(r["y"], np.float32) for r in res.results], axis=0)
    return out
```
